# Optimizing a Trainium2 kernel written in Bass

```python
import jax, jax.numpy as jnp
from jax import lax
import numpy as np

D_MODEL = 1024
BATCH = 32
SEQ = 256
DEPTH = 1
DEC_BATCH = 4
DEC_SEQ = 4096
PAST_LEN = 256

GRID_W = 64
FFN_DIM = 2816
GMLP_GROUPS = 8
GMLP_GROUP_DIM = 128
GMLP_DIM = GMLP_GROUPS * GMLP_GROUP_DIM
CHUNK = 128
MLA_HEADS = 8
QK_NOPE_DIM = 128
QK_ROPE_DIM = 64
V_HEAD_DIM = 128
Q_LORA_RANK = 256
KV_LORA_RANK = 256
MLA_DIM = MLA_HEADS * V_HEAD_DIM
ROPE_BASE = 10000.0
Q_BLOCK = 128
N_MOD = 9
EPS = 1e-6
IN_DIM = 2 * GMLP_DIM + Q_LORA_RANK + KV_LORA_RANK + QK_ROPE_DIM + 2 * D_MODEL

kernel_name = "hybrid_gmlp_mla_macaron_diffusion_step"


def rms_norm(x, g):
    xf = x.astype(jnp.float32)
    y = xf * lax.rsqrt(jnp.mean(xf * xf, axis=-1, keepdims=True) + EPS)
    return (y * g.astype(jnp.float32)).astype(x.dtype)


def modulate(x, shift, scale):
    return x * (1 + scale[:, None, :]) + shift[:, None, :]


def adaln(cond, mod_w, mod_b):
    m = jax.nn.silu(cond) @ mod_w + mod_b
    return m.reshape(cond.shape[0], N_MOD, D_MODEL)


def swiglu(h, w_in, w_out):
    g, u = jnp.split(h @ w_in, 2, axis=-1)
    return (jax.nn.silu(g) * u) @ w_out


def axial_angles(L):
    rows = L // GRID_W
    r = jnp.repeat(jnp.arange(rows, dtype=jnp.float32), GRID_W)
    col = jnp.tile(jnp.arange(GRID_W, dtype=jnp.float32), rows)
    half = QK_ROPE_DIM // 2
    inv = 1.0 / (ROPE_BASE ** (jnp.arange(0, half, 2, dtype=jnp.float32) / half))
    return r[:, None] * inv, col[:, None] * inv


def rope_half(x, ang):
    x1, x2 = jnp.split(x, 2, axis=-1)
    cos, sin = jnp.cos(ang), jnp.sin(ang)
    return jnp.concatenate([x1 * cos - x2 * sin, x2 * cos + x1 * sin], axis=-1)


def axial_rope(x, ang_r, ang_c):
    xf = x.astype(jnp.float32)
    xr, xc = jnp.split(xf, 2, axis=-1)
    return jnp.concatenate([rope_half(xr, ang_r), rope_half(xc, ang_c)], axis=-1).astype(x.dtype)


def chunk_gmlp(u, v, v_norm, w_s, b_s):
    B, L, _ = u.shape
    nc = L // CHUNK
    vv = rms_norm(v, v_norm).reshape(B, nc, CHUNK, GMLP_GROUPS, GMLP_GROUP_DIM)
    mixed = jnp.einsum('gpq,bcqgd->bcpgd', w_s, vv) + b_s[:, :, None]
    return u * mixed.reshape(B, L, GMLP_DIM)


def q_up(q_lat, q_norm, w_q_up):
    B, L, _ = q_lat.shape
    q = (rms_norm(q_lat, q_norm) @ w_q_up).reshape(B, L, MLA_HEADS, QK_NOPE_DIM + QK_ROPE_DIM)
    return q[..., :QK_NOPE_DIM], q[..., QK_NOPE_DIM:]


def kv_up(ckv, w_kv_up):
    B, L, _ = ckv.shape
    kv = (ckv @ w_kv_up).reshape(B, L, MLA_HEADS, QK_NOPE_DIM + V_HEAD_DIM)
    return kv[..., :QK_NOPE_DIM], kv[..., QK_NOPE_DIM:]


def attend_block(qn, qr, kn, kr, v):
    s = jnp.einsum('bqhd,bkhd->bhqk', qn, kn) + jnp.einsum('bqhr,bkr->bhqk', qr, kr)
    s = s.astype(jnp.float32) * ((QK_NOPE_DIM + QK_ROPE_DIM) ** -0.5)
    p = jax.nn.softmax(s, axis=-1).astype(v.dtype)
    return jnp.einsum('bhqk,bkhd->bqhd', p, v)


def mla_attention(q_nope, q_rope, k_nope, k_rope, v):
    B, L = q_nope.shape[:2]
    nb = L // Q_BLOCK

    def blocks(t):
        return jnp.moveaxis(t.reshape(B, nb, Q_BLOCK, *t.shape[2:]), 1, 0)

    out = lax.map(lambda qs: attend_block(qs[0], qs[1], k_nope, k_rope, v),
                  (blocks(q_nope), blocks(q_rope)))
    return jnp.moveaxis(out, 0, 1).reshape(B, L, MLA_DIM)


def trunk_layer(x, mods, lw, ctx_ckv, ctx_krope):
    sh1, sc1, g1, sh2, sc2, g2, sh3, sc3, g3 = [mods[:, i] for i in range(N_MOD)]
    B, L, _ = x.shape
    h = modulate(rms_norm(x, lw['norm_ffn1']), sh1, sc1)
    x = x + 0.5 * g1[:, None, :] * swiglu(h, lw['ffn1_w_in'], lw['ffn1_w_out'])
    h = modulate(rms_norm(x, lw['norm_mix']), sh2, sc2)
    proj = h @ lw['w_in']
    offs = np.cumsum([GMLP_DIM, GMLP_DIM, Q_LORA_RANK, KV_LORA_RANK, QK_ROPE_DIM]).tolist()
    u, v, q_lat, ckv_raw, krope, gate_logits = jnp.split(proj, offs, axis=-1)
    out_a = chunk_gmlp(u, v, lw['gmlp_v_norm'], lw['gmlp_w_s'], lw['gmlp_b_s'])
    ckv = rms_norm(ckv_raw, lw['kv_norm'])
    q_nope, q_rope = q_up(q_lat, lw['q_norm'], lw['w_q_up'])
    k_nope, vals = kv_up(ckv, lw['w_kv_up'])
    k_rope = krope
    if ctx_ckv is not None:
        ang_r, ang_c = axial_angles(L)
        q_rope = axial_rope(q_rope, ang_r[:, None, :], ang_c[:, None, :])
        k_rope = axial_rope(krope, ang_r, ang_c)
        ck_nope, c_vals = kv_up(ctx_ckv, lw['w_kv_up'])
        k_nope = jnp.concatenate([ck_nope, k_nope], axis=1)
        vals = jnp.concatenate([c_vals, vals], axis=1)
        k_rope = jnp.concatenate([ctx_krope, k_rope], axis=1)
    out_b = mla_attention(q_nope, q_rope, k_nope, k_rope, vals)
    ga, gb = jnp.split(jax.nn.sigmoid(gate_logits), 2, axis=-1)
    merged = (ga * (out_a @ lw['w_a_proj']) + gb * (out_b @ lw['w_b_proj'])) @ lw['w_o']
    x = x + g2[:, None, :] * merged
    h = modulate(rms_norm(x, lw['norm_ffn2']), sh3, sc3)
    x = x + 0.5 * g3[:, None, :] * swiglu(h, lw['ffn2_w_in'], lw['ffn2_w_out'])
    return x, ckv, krope


def setup_inputs(seed: int = 0) -> dict:
    key = jax.random.key(seed)
    ks = iter(jax.random.split(key, 40))

    def nrm(shape, scale=1.0):
        return jax.random.normal(next(ks), shape, jnp.float32) * scale

    def gain(n):
        return 1.0 + nrm((DEPTH, n), 0.01)

    return {
        'x_prompt': nrm((BATCH, SEQ, D_MODEL)),
        'x_sample': nrm((DEC_BATCH, DEC_SEQ, D_MODEL)),
        'c': nrm((DEC_BATCH, D_MODEL)),
        'cache_ckv': nrm((DEC_BATCH, DEPTH, PAST_LEN, KV_LORA_RANK)),
        'cache_krope': nrm((DEC_BATCH, DEPTH, PAST_LEN, QK_ROPE_DIM)),
        'c_ctx': nrm((D_MODEL,)),
        'mod_w': nrm((DEPTH, D_MODEL, N_MOD * D_MODEL), 0.5 * D_MODEL ** -0.5),
        'mod_b': nrm((DEPTH, N_MOD * D_MODEL), 0.01),
        'norm_ffn1': gain(D_MODEL),
        'ffn1_w_in': nrm((DEPTH, D_MODEL, 2 * FFN_DIM), D_MODEL ** -0.5),
        'ffn1_w_out': nrm((DEPTH, FFN_DIM, D_MODEL), FFN_DIM ** -0.5),
        'norm_mix': gain(D_MODEL),
        'w_in': nrm((DEPTH, D_MODEL, IN_DIM), D_MODEL ** -0.5),
        'gmlp_v_norm': gain(GMLP_DIM),
        'gmlp_w_s': nrm((DEPTH, GMLP_GROUPS, CHUNK, CHUNK), CHUNK ** -0.5),
        'gmlp_b_s': 1.0 + nrm((DEPTH, CHUNK, GMLP_GROUPS), 0.01),
        'q_norm': gain(Q_LORA_RANK),
        'w_q_up': nrm((DEPTH, Q_LORA_RANK, MLA_HEADS * (QK_NOPE_DIM + QK_ROPE_DIM)), Q_LORA_RANK ** -0.5),
        'kv_norm': gain(KV_LORA_RANK),
        'w_kv_up': nrm((DEPTH, KV_LORA_RANK, MLA_HEADS * (QK_NOPE_DIM + V_HEAD_DIM)), KV_LORA_RANK ** -0.5),
        'w_a_proj': nrm((DEPTH, GMLP_DIM, D_MODEL), GMLP_DIM ** -0.5),
        'w_b_proj': nrm((DEPTH, MLA_DIM, D_MODEL), MLA_DIM ** -0.5),
        'w_o': nrm((DEPTH, D_MODEL, D_MODEL), D_MODEL ** -0.5),
        'norm_ffn2': gain(D_MODEL),
        'ffn2_w_in': nrm((DEPTH, D_MODEL, 2 * FFN_DIM), D_MODEL ** -0.5),
        'ffn2_w_out': nrm((DEPTH, FFN_DIM, D_MODEL), FFN_DIM ** -0.5),
        'norm_final': 1.0 + nrm((D_MODEL,), 0.01),
    }


def reference(x_prompt, x_sample, c, cache_ckv, cache_krope, c_ctx, mod_w, mod_b,
              norm_ffn1, ffn1_w_in, ffn1_w_out, norm_mix, w_in, gmlp_v_norm, gmlp_w_s,
              gmlp_b_s, q_norm, w_q_up, kv_norm, w_kv_up, w_a_proj, w_b_proj, w_o,
              norm_ffn2, ffn2_w_in, ffn2_w_out, norm_final):
    xp, xs = x_prompt, x_sample
    ckv_list, krope_list = [], []
    for l in range(DEPTH):
        lw = {
            'norm_ffn1': norm_ffn1[l], 'ffn1_w_in': ffn1_w_in[l], 'ffn1_w_out': ffn1_w_out[l],
            'norm_mix': norm_mix[l], 'w_in': w_in[l], 'gmlp_v_norm': gmlp_v_norm[l],
            'gmlp_w_s': gmlp_w_s[l], 'gmlp_b_s': gmlp_b_s[l], 'q_norm': q_norm[l],
            'w_q_up': w_q_up[l], 'kv_norm': kv_norm[l], 'w_kv_up': w_kv_up[l],
            'w_a_proj': w_a_proj[l], 'w_b_proj': w_b_proj[l], 'w_o': w_o[l],
            'norm_ffn2': norm_ffn2[l], 'ffn2_w_in': ffn2_w_in[l], 'ffn2_w_out': ffn2_w_out[l],
        }
        mods_ctx = adaln(c_ctx[None, :], mod_w[l], mod_b[l])
        xp, ckv_l, krope_l = trunk_layer(xp, mods_ctx, lw, None, None)
        ckv_list.append(ckv_l)
        krope_list.append(krope_l)
        mods_lat = adaln(c, mod_w[l], mod_b[l])
        xs, _, _ = trunk_layer(xs, mods_lat, lw, cache_ckv[:, l], cache_krope[:, l])
    y_prompt = rms_norm(xp, norm_final)
    y_sample = rms_norm(xs, norm_final)
    new_ckv = jnp.stack(ckv_list, axis=1)
    new_krope = jnp.stack(krope_list, axis=1)
    return (y_prompt, y_sample, new_ckv, new_krope)
```

```python
import numpy as np
from contextlib import ExitStack
import concourse.bass as bass
import concourse.mybir as mybir
from concourse.bass_utils import run_bass_kernel_spmd

F32 = mybir.dt.float32
BF16 = mybir.dt.bfloat16
AF = mybir.ActivationFunctionType
ALU = mybir.AluOpType

ENGS = ("pe", "act", "dve", "pool", "sp")
EPS = 1e-6
T = 512
SCALE = 192.0 ** -0.5


class Prog:
    def __init__(self, nc, same_engine_sync=True):
        self.nc = nc
        self.ops = []
        self.last_write = {}
        self.readers = {}
        self.same_engine_sync = same_engine_sync
        self.group_keys = set()

    def add(self, eng, fn, reads=(), writes=(), dma=None, group=False):
        i = len(self.ops)
        deps = set()
        for r in reads:
            lw = self.last_write.get(r)
            if lw is not None:
                deps.add(lw)
        for w in writes:
            lw = self.last_write.get(w)
            if lw is not None:
                deps.add(lw)
            rd = self.readers.get(w)
            if rd:
                for v in rd.values():
                    if isinstance(v, list):
                        deps.update(v)
                    else:
                        deps.add(v)
        is_dma = dma is not None
        for r in reads:
            rd = self.readers.setdefault(r, {})
            if is_dma:
                rd.setdefault("_dma", []).append(i)
            else:
                rd[eng] = i
        for w in writes:
            self.last_write[w] = i
            self.readers[w] = {}
        deps.discard(i)
        if group:
            self.group_keys.add(dma)
        self.ops.append(dict(eng=eng, fn=fn, deps=deps, dma=dma))
        return i

    def emit(self, stack):
        nc = self.nc
        ops = self.ops
        dma_cnt = {}
        for op in ops:
            if op["dma"] is not None:
                k = op["dma"]
                dma_cnt[k] = dma_cnt.get(k, 0) + 16
                op["dma_val"] = dma_cnt[k]
        for op in ops:
            if op["dma"] is not None and op["dma"] in self.group_keys:
                op["dma_val"] = dma_cnt[op["dma"]]
        for i, op in enumerate(ops):
            eng_dep = {}
            dma_dep = {}
            for j in op["deps"]:
                d = ops[j]
                if d["dma"] is not None:
                    k = d["dma"]
                    dma_dep[k] = max(dma_dep.get(k, 0), d["dma_val"])
                else:
                    e2 = d["eng"]
                    if e2 == op["eng"] and op["dma"] is None:
                        if e2 == "pe" or not self.same_engine_sync:
                            continue
                    eng_dep[e2] = max(eng_dep.get(e2, -1), j)
            op["eng_dep"] = eng_dep
            op["dma_dep"] = dma_dep
            for e2, j in eng_dep.items():
                ops[j]["mark"] = True
        cnt = {e: 0 for e in ENGS}
        for op in ops:
            if op.get("mark"):
                cnt[op["eng"]] += 1
                op["val"] = cnt[op["eng"]]
        esem = {e: stack.enter_context(nc.semaphore("s_" + e)) for e in ENGS}
        dsem = {}
        for n, k in enumerate(dma_cnt):
            dsem[k] = stack.enter_context(nc.semaphore("d%d" % n))
        block = stack.enter_context(nc.Block())

        def run(eng):
            def body(e):
                waited = {}
                for op in ops:
                    if op["eng"] != eng:
                        continue
                    for e2, j in op["eng_dep"].items():
                        v = ops[j]["val"]
                        key = ("e", e2)
                        if waited.get(key, 0) < v:
                            e.wait_ge(esem[e2], v)
                            waited[key] = v
                    for k, v in op["dma_dep"].items():
                        key = ("d", k)
                        if waited.get(key, 0) < v:
                            e.wait_ge(dsem[k], v)
                            waited[key] = v
                    ins = op["fn"](e)
                    if op["dma"] is not None:
                        ins.then_inc(dsem[op["dma"]], 16)
                    elif op.get("mark"):
                        ins.then_inc(esem[eng], 1)
                if eng == "sp":
                    for k, v in dma_cnt.items():
                        e.wait_ge(dsem[k], v)
            return body

        block.sync(run("sp"))
        block.scalar(run("act"))
        block.vector(run("dve"))
        block.gpsimd(run("pool"))
        block.tensor(run("pe"))


S_X = 0
S_H = 16
S_A = 24
S_Y = 24
S_XIN = 40
S_QLR = 24
S_CKR = 28
S_U = 24
S_QLN = 32
S_CKN = 34
S_QN = 36
S_VV = 36
S_QR = 44
S_MRG = 44
S_SQ = 52
S_PT = 52
S_OB = 56
S_RSTD = 60
S_RT = 62
S_TMP0 = 64
S_TMP1 = 66
S_SG0 = 68
S_SG1 = 70
S_OUTB = 72
S_KRF = 80
S_OST = 82
S_OST2 = 86
S_X2 = 87
N_SLOTS = 103
NW = 4


def build_program(NPH, NOH, NOX, PAST=256):
    NK = PAST + (NOH + NOX) * T
    NKT = NK // 128
    nc = bass.Bass("TRN2", target_bir_lowering=False)
    D = {}

    def din(name, shape):
        D[name] = nc.dram_tensor(name, list(shape), F32, kind="ExternalInput").ap()
        return D[name]

    def dout(name, shape):
        D[name] = nc.dram_tensor(name, list(shape), F32, kind="ExternalOutput").ap()
        return D[name]

    din("modw", [18, 128, 4096]); din("modb", [128, 72]); din("gains", [128, 32]); din("gains2", [128, 4])
    din("gvbc", [128, 1024]); din("bsbc", [128, 1024]); din("gfbc", [128, 1024]); din("wsT", [128, 1024]); din("ident", [128, 128])
    din("f1w1", [11, 128, 4096]); din("f1w2", [8, 128, 2816]); din("f2w1", [11, 128, 4096]); din("f2w2", [8, 128, 2816])
    din("wu", [2, 128, 4096]); din("wv", [2, 128, 4096]); din("wql", [128, 2048]); din("wck", [128, 2048])
    din("wkr", [128, 1536]); din("wg", [4, 128, 4096]); din("wq", [128, 4096]); din("wkv", [128, 4096])
    din("wa", [2, 128, 4096]); din("wb", [2, 128, 4096]); din("wo", [2, 128, 4096])
    din("xp", [NPH * T, 1024]); din("xso", [NOH * T, 1024]); din("xsx", [NOX * T, 1024])
    din("cv", [128, 16]); din("cck", [PAST, 256]); din("ckr", [PAST, 64])
    din("cos_o", [64, NOH * T]); din("sin_o", [64, NOH * T]); din("cos_x", [64, NOX * T]); din("sin_x", [64, NOX * T])
    dout("yp", [NPH * T, 1024]); dout("ys", [NOH * T, 1024]); dout("nckv", [NPH * T, 256]); dout("nkr", [NPH * T, 64])
    xs1 = nc.dram_tensor("xs1", [NOH, 128, 4096], F32, kind="Internal").ap()

    with ExitStack() as st:
        def sb(name, shape, dt):
            return st.enter_context(nc.sbuf_tensor("sb_" + name, list(shape), dt))

        arena = sb("arena", [128, N_SLOTS * 512], BF16)
        wsl = [sb("wsl%d" % i, [128, 4096], BF16) for i in range(NW)]
        wkv = sb("wkv", [128, 4096], BF16)
        ident = sb("ident_sb", [128, 128], F32)
        identb = sb("identb", [128, 128], BF16)
        onesb = sb("onesb", [128, 128], BF16)
        epst = sb("epst", [128, 1], F32)
        modb = sb("modb_sb", [128, 72], F32)
        mods = sb("mods", [128, 144], F32)
        gains = sb("gains_sb", [128, 32], F32)
        gains2 = sb("gains2_sb", [128, 4], F32)
        dc = sb("dc", [128, 2 * 3 * 3 * 8], F32)
        cvt = sb("cvt", [128, 16], F32)
        scond = sb("scond", [128, 16], BF16)
        gvbc = sb("gvbc_sb", [128, 1024], F32)
        bsbc = sb("bsbc_sb", [128, 1024], F32)
        wsT = sb("wsT_sb", [128, 1024], BF16)
        kK2 = sb("kK2", [128, NK], BF16)
        small = sb("small", [128, 32], F32)
        ckvT = sb("ckvT", [128, 2 * NK], BF16)
        krT = sb("krT", [128, NK], BF16)
        krTp = sb("krTp", [128, 512], BF16)
        kvb1 = sb("kvb", [128, max(NK + NKT * 128, 8192)], BF16)
        kvb = [kvb1, kvb1]
        ps = [st.enter_context(nc.psum_tensor("ps%d" % i, [128, 512], F32)) for i in range(8)]

        P = Prog(nc)
        bank_rr = [0]
        reserved = set()

        def nb():
            while True:
                b = bank_rr[0] % 8
                bank_rr[0] += 1
                if b not in reserved:
                    return b

        w_rr = [0]

        class _V:
            def __init__(self, ap):
                self.ap = ap

            def __getitem__(self, idx):
                return self.ap[idx]

        cost = _V(arena[0:64, 80 * 512:82 * 512].bitcast(F32))
        sint = _V(arena[0:64, 82 * 512:84 * 512].bitcast(F32))

        def SL(i, n=1, p0=0, p1=128):
            return arena[p0:p1, i * 512:(i + n) * 512]

        def SLf(i, p0=0, p1=128):
            return arena[p0:p1, i * 512:(i + 2) * 512].bitcast(F32)

        def K(i, n=1):
            return [("s", i + k) for k in range(n)]

        def PK(b):
            return [("p", b)]

        def MM(out, lhsT, rhs, start, stop, rd, wr, skip=False):
            P.add("pe", lambda e: e.matmul(out, lhsT=lhsT, rhs=rhs, start=start, stop=stop, skip_group_check=skip), rd, wr)

        def TR(out, in_, idn, rd, wr):
            P.add("pe", lambda e: e.transpose(out=out, in_=in_, identity=idn), rd, wr)

        def ACT(out, in_, func, rd, wr, scale=None, bias=None, accum=None):
            kw = {}
            if scale is not None:
                kw["scale"] = scale
            if bias is not None:
                kw["bias"] = bias
            if accum is not None:
                kw["accum_out"] = accum
            P.add("act", lambda e: e.activation(out=out, in_=in_, func=func, **kw), rd, wr)

        def CP(eng, out, in_, rd, wr):
            if eng == "act":
                P.add("act", lambda e: e.copy(out=out, in_=in_), rd, wr)
            else:
                P.add(eng, lambda e: e.tensor_copy(out=out, in_=in_), rd, wr)

        def TT(eng, out, in0, in1, op, rd, wr):
            P.add(eng, lambda e: e.tensor_tensor(out=out, in0=in0, in1=in1, op=op), rd, wr)

        def STT(out, in0, scalar, in1, op0, op1, rd, wr, eng="dve"):
            P.add(eng, lambda e: e.scalar_tensor_tensor(out=out, in0=in0, scalar=scalar, in1=in1, op0=op0, op1=op1), rd, wr)

        def TS(out, in0, s1, s2, op0, op1, rd, wr, eng="dve"):
            if s2 is None:
                P.add(eng, lambda e: e.tensor_scalar(out=out, in0=in0, scalar1=s1, scalar2=None, op0=op0), rd, wr)
            else:
                P.add(eng, lambda e: e.tensor_scalar(out=out, in0=in0, scalar1=s1, scalar2=s2, op0=op0, op1=op1), rd, wr)

        def RCP(out, in_, rd, wr):
            P.add("dve", lambda e: e.reciprocal(out=out, in_=in_), rd, wr)

        def MEMSET(eng, ap, val, wr):
            P.add(eng, lambda e: e.memset(ap, val), (), wr)

        def DMA(q, out, in_, rd, wr, key, group=False):
            P.add(q, lambda e: e.dma_start(out=out, in_=in_), rd, wr, dma=key, group=group)

        def WLOAD(src, n=4096):
            s = w_rr[0] % NW
            w_rr[0] += 1
            DMA("pool", wsl[s][:, 0:n], src, (), [("w", s)], ("w", s))
            return wsl[s], [("w", s)]

        def const_load(tile_ap, src, key):
            DMA("sp", tile_ap, src, (), [key], "const", group=True)

        const_load(ident[:], D["ident"][:, :], "ident")
        const_load(modb[:], D["modb"][:, :], "modb")
        const_load(gains[:], D["gains"][:, :], "gains")
        const_load(gains2[:], D["gains2"][:, :], "gains2")
        const_load(cvt[:], D["cv"][:, :], "cvt")
        const_load(gvbc[:], D["gvbc"][:, :], "gvbc")
        const_load(bsbc[:], D["bsbc"][:, :], "bsbc")
        DMA("pool", wkv[:], D["wkv"][:, :], (), ["wkv"], "wkv")
        DMA("pool", wsT[:], D["wsT"][:, :], (), ["wsT"], "wsT")
        CP("dve", identb[:], ident[:], ["ident"], ["identb"])
        MEMSET("dve", onesb[:], 1.0, ["onesb"])
        MEMSET("dve", krT[64:128, :], 0.0, ["krTz"])
        MEMSET("dve", krTp[64:128, :], 0.0, ["krTpz"])
        MEMSET("dve", epst[:], EPS, ["epst"])
        ACT(scond[:], cvt[:], AF.Silu, ["cvt"], ["scond"])
        scv = scond[:].rearrange("p (k c) -> p k c", c=2)
        mb = nb()
        for blk in range(18):
            wt, wk = WLOAD(D["modw"][blk])
            wv_ = wt[:].rearrange("p (k c) -> p k c", k=8)
            for cc in range(4):
                q = blk * 4 + cc
                for kc in range(8):
                    MM(ps[mb][:, q * 2:q * 2 + 2], wv_[:, kc, cc * 128:(cc + 1) * 128], scv[:, kc, :], kc == 0, kc == 7,
                       wk + ["scond"], PK(mb))
        m3 = mods[:].rearrange("p (q c) -> p q c", c=2)
        TT("dve", m3, ps[mb][:, 0:144].rearrange("p (q c) -> p q c", c=2), modb[:].unsqueeze(2).to_broadcast([128, 72, 2]),
           ALU.add, PK(mb) + ["modb"], ["mods"])
        dcv = dc[:].rearrange("p (c n k m) -> p c n k m", c=2, n=3, k=3)
        for cd in range(2):
            for n in range(3):
                sh = m3[:, (3 * n) * 8:(3 * n) * 8 + 8, cd]
                sc = m3[:, (3 * n + 1) * 8:(3 * n + 1) * 8 + 8, cd]
                gt = m3[:, (3 * n + 2) * 8:(3 * n + 2) * 8 + 8, cd]
                STT(dcv[:, cd, n, 0, :], sc, 1.0, gains[:, n * 8:(n + 1) * 8], ALU.add, ALU.mult, ["mods", "gains"], ["dc"])
                CP("dve", dcv[:, cd, n, 1, :], sh, ["mods"], ["dc"])
                TS(dcv[:, cd, n, 2, :], gt, 0.5 if n != 1 else 1.0, None, ALU.mult, None, ["mods"], ["dc"])

        def dcs(cd, n, k, m):
            return dcv[:, cd, n, k, m:m + 1]

        XB = [S_X, S_X2]
        par = [0]

        def xs(m):
            return SLf(XB[par[0]] + 2 * m)

        def xk(m):
            return K(XB[par[0]] + 2 * m, 2)

        def xall():
            return arena[:, XB[par[0]] * 512:(XB[par[0]] + 16) * 512].bitcast(F32)

        def xallk():
            return K(XB[par[0]], 16)

        def hs(kc):
            return SL(S_H + kc)

        def load_x(src_rows):
            sb_ = stat_begin()
            x3 = xall().rearrange("p (m t) -> p m t", m=8)
            for tt in range(4):
                si = S_TMP0 + 4 * (tt % 2)
                stg_full = arena[:, si * 512:(si + 4) * 512].bitcast(F32)
                sk = K(si, 4)
                DMA("sp", stg_full, src_rows[tt * 128:(tt + 1) * 128, :], (), sk, ("xi", tt % 2))
                KLX = 9
                for half in range(2 if KLX >= 1 else 0):
                    b = nb()
                    for k in range(4):
                        m = half * 4 + k
                        TR(ps[b][:, k * 128:(k + 1) * 128], stg_full[:, m * 128:(m + 1) * 128], ident[:], sk + ["ident"], PK(b))
                    for k in range(4 if KLX >= 2 else 0):
                        m = half * 4 + k
                        if KLX == 2:
                            eng = "act"
                        elif KLX == 3:
                            eng = "dve"
                        else:
                            eng = "act" if half == 0 else "dve"
                        CP(eng, xs(m)[:, tt * 128:(tt + 1) * 128], ps[b][:, k * 128:(k + 1) * 128], PK(b), xk(m))
                if KLX >= 2:
                    sq3 = SL(S_SQ + 2 * tt, 2).rearrange("p (m t) -> p m t", m=8)
                    ACT(sq3, x3[:, :, tt * 128:(tt + 1) * 128], AF.Square, xallk(), K(S_SQ + 2 * tt, 2))
                    for m in range(8):
                        MM(ps[sb_][:, tt * 128:(tt + 1) * 128], onesb[:], sq3[:, m, :], m == 0, m == 7, K(S_SQ + 2 * tt, 2) + ["onesb"], PK(sb_))

        def stats(sq_aps, sq_keys, nfeat, out_rstd, out_key):
            b = nb()
            n = len(sq_aps)
            for i, a in enumerate(sq_aps):
                MM(ps[b][:, :], onesb[:], a, i == 0, i == n - 1, sq_keys[i] + ["onesb"], PK(b))
            ACT(SLf(S_RT), ps[b][:, :], AF.Sqrt, PK(b) + ["epst"], K(S_RT, 2), scale=1.0 / nfeat, bias=epst[:, 0:1])
            RCP(out_rstd, SLf(S_RT), K(S_RT, 2), out_key)

        pstat = [None, None]

        def stat_begin():
            b = nb()
            reserved.add(b)
            pstat[par[0]] = b
            return b

        def stat_chunk(b, m, first, last):
            ACT(SL(S_SQ + m), xs(m), AF.Square, xk(m), K(S_SQ + m))
            return (b, m, first, last)

        def stat_mm(item):
            b, m, first, last = item
            MM(ps[b][:, :], onesb[:], SL(S_SQ + m), first, last, K(S_SQ + m) + ["onesb"], PK(b))

        early_rstd = [False, False]

        def rstd_from_pending(early=False):
            b = pstat[par[0]]
            rt, rs = (48, 46) if early else (S_RT, S_RSTD)
            ACT(SLf(rt), ps[b][:, :], AF.Sqrt, PK(b) + ["epst"], K(rt, 2), scale=1.0 / 1024.0, bias=epst[:, 0:1])
            RCP(SLf(rs), SLf(rt), K(rt, 2), K(rs, 2))
            reserved.discard(b)
            pstat[par[0]] = None
            early_rstd[par[0]] = early

        def norm_mod(cd, n):
            if early_rstd[par[0]]:
                rs = 46
                early_rstd[par[0]] = False
            else:
                rstd_from_pending()
                rs = S_RSTD
            for m in range(8):
                tk = S_TMP0 if m % 2 == 0 else S_TMP1
                TT("dve", SLf(tk), xs(m), SLf(rs), ALU.mult, xk(m) + K(rs, 2), K(tk, 2))
                ACT(hs(m), SLf(tk), AF.Identity, K(tk, 2) + ["dc"], K(S_H + m), scale=dcs(cd, n, 0, m), bias=dcs(cd, n, 1, m))

        def lin_multi(specs, KC, rhs_fn, rhs_keys):
            banks = [nb() for _ in specs]
            for kc in range(KC):
                for (wv_, wk, col0, ncol), b in zip(specs, banks):
                    MM(ps[b][0:ncol, :], wv_[:, kc, col0:col0 + ncol], rhs_fn(kc), kc == 0, kc == KC - 1,
                       wk + (rhs_keys(kc) if callable(rhs_keys) else rhs_keys), PK(b))
            return banks

        def ffn(cd, n, w1, w2, stats_after=True, mid_hook=None, skip_norm=False):
            if not skip_norm:
                norm_mod(cd, n)
            hk = lambda kc: K(S_H + kc)
            for blk in range(11):
                wt, wk = WLOAD(w1[blk])
                wv_ = wt[:].rearrange("p (k g c) -> p k (g c)", k=8, g=2)
                bks = lin_multi([(wv_, wk, gu * 256 + jj * 128, 128) for jj in range(2) for gu in range(2)], 8, hs, hk)
                for jj in range(2):
                    j = 2 * blk + jj
                    bg, bu = bks[jj * 2], bks[jj * 2 + 1]
                    sg = S_SG0 if j % 2 == 0 else S_SG1
                    ACT(SLf(sg), ps[bg][:, :], AF.Silu, PK(bg), K(sg, 2))
                    TT("dve", SL(S_A + j), SLf(sg), ps[bu][:, :], ALU.mult, K(sg, 2) + PK(bu), K(S_A + j))
            if stats_after:
                ACT(small[:, 30:31], epst[:, 0:1], AF.Sqrt, ["epst"], [("small", 30)])
            if mid_hook is not None:
                mid_hook()
            sb_ = stat_begin() if stats_after else None
            pend = None
            for m in range(8):
                wt, wk = WLOAD(w2[m], 2816)
                wv_ = wt[:, 0:2816].rearrange("p (j c) -> p j c", j=22)
                b = nb()
                for j in range(22):
                    MM(ps[b][:, :], wv_[:, j, :], SL(S_A + j), j == 0, j == 21, wk + K(S_A + j), PK(b))
                if pend is not None:
                    stat_mm(pend)
                STT(xs(m), ps[b][:, :], dcs(cd, n, 2, m), xs(m), ALU.mult, ALU.add, PK(b) + xk(m) + ["dc"], xk(m))
                if stats_after:
                    pend = stat_chunk(sb_, m, m == 0, m == 7)
            if pend is not None:
                stat_mm(pend)

        def lin_chunk(wv_, wk, col0, ncol, KC, rhs_fn, rhs_keys, bank=None):
            b = nb() if bank is None else bank
            for kc in range(KC):
                MM(ps[b][0:ncol, :], wv_[:, kc, col0:col0 + ncol], rhs_fn(kc), kc == 0, kc == KC - 1,
                   wk + (rhs_keys(kc) if callable(rhs_keys) else rhs_keys), PK(b))
            return b

        def qk_latent(cd, sample, koff=None, want_q=True, want_kv=True, cos_src=None, sin_src=None, g=None, mid_hook=None):
            hk = lambda kc: K(S_H + kc)
            specs = []
            if want_q:
                wt, wkq = WLOAD(D["wql"][:, :], 2048)
                wvq = wt[:, 0:2048].rearrange("p (k c) -> p k c", k=8)
                specs += [(wvq, wkq, 0, 128), (wvq, wkq, 128, 128)]
            if want_kv:
                wt, wkc = WLOAD(D["wck"][:, :], 2048)
                wvc = wt[:, 0:2048].rearrange("p (k c) -> p k c", k=8)
                specs += [(wvc, wkc, 0, 128), (wvc, wkc, 128, 128)]
                wt, wkr_ = WLOAD(D["wkr"][:, :], 1536)
                wvr = wt[:, 0:1536].rearrange("p (k c) -> p k c", k=8)
                if not want_q:
                    specs += [(wvr, wkr_, 0, 128), (wvr, wkr_, 64, 128)]
            banks = lin_multi(specs, 8, hs, hk)
            if want_q and want_kv:
                b1 = lin_chunk(wvr, wkr_, 0, 128, 8, hs, hk)
            hooked = [False]

            def run_hook():
                if mid_hook is not None and not hooked[0]:
                    hooked[0] = True
                    mid_hook()

            if want_q:
                for c in range(2):
                    b = banks.pop(0)
                    CP("act", SLf(S_QLR + 2 * c), ps[b][:, :], PK(b), K(S_QLR + 2 * c, 2))
                    ACT(SL(S_SQ + c), ps[b][:, :], AF.Square, PK(b), K(S_SQ + c))
                stats([SL(S_SQ + c) for c in range(2)], [K(S_SQ + c) for c in range(2)], 256.0, SLf(S_RSTD), K(S_RSTD, 2))
                for c in range(2):
                    STT(SL(S_QLN + c), SLf(S_QLR + 2 * c), gains2[:, c:c + 1], SLf(S_RSTD), ALU.mult, ALU.mult,
                        K(S_QLR + 2 * c, 2) + K(S_RSTD, 2) + ["gains2"], K(S_QLN + c))
            if want_kv:
                for c in range(2):
                    b = banks.pop(0)
                    CP("act", SLf(S_CKR + 2 * c), ps[b][:, :], PK(b), K(S_CKR + 2 * c, 2))
                    ACT(SL(S_SQ + 2 + c), ps[b][:, :], AF.Square, PK(b), K(S_SQ + 2 + c))
                if sample and not want_q:
                    b1 = banks.pop(0)
                    b2 = banks.pop(0)
                    DMA("sp", cost[:, :], cos_src, (), K(80, 2), "cost")
                    DMA("sp", sint[:, :], sin_src, (), K(82, 2), "sint")
                    TT("dve", SLf(S_TMP0, 0, 64), ps[b1][0:64, :], cost[:, :], ALU.mult, PK(b1) + K(80, 2), K(S_TMP0, 2))
                    TT("dve", SLf(S_SG0, 0, 64), ps[b2][0:64, :], sint[:, :], ALU.mult, PK(b2) + K(82, 2), K(S_SG0, 2))
                    TT("dve", krT[0:64, koff:koff + T], SLf(S_TMP0, 0, 64), SLf(S_SG0, 0, 64), ALU.add, K(S_TMP0, 2) + K(S_SG0, 2), [("krT", koff)])
                    run_hook()
                stats([SL(S_SQ + 2 + c) for c in range(2)], [K(S_SQ + 2 + c) for c in range(2)], 256.0, SLf(S_TMP1), K(S_TMP1, 2))
                for c in range(2):
                    if sample:
                        STT(ckvT[:, c * NK + koff:c * NK + koff + T], SLf(S_CKR + 2 * c), gains2[:, 2 + c:3 + c], SLf(S_TMP1), ALU.mult, ALU.mult,
                            K(S_CKR + 2 * c, 2) + K(S_TMP1, 2) + ["gains2"], [("ckvT", koff)])
                    else:
                        STT(SLf(S_CKR + 2 * c), SLf(S_CKR + 2 * c), gains2[:, 2 + c:3 + c], SLf(S_TMP1), ALU.mult, ALU.mult,
                            K(S_CKR + 2 * c, 2) + K(S_TMP1, 2) + ["gains2"], K(S_CKR + 2 * c, 2))
                        CP("act", SL(S_CKN + c), SLf(S_CKR + 2 * c), K(S_CKR + 2 * c, 2), K(S_CKN + c))
                if sample and not want_q:
                    return
                if sample:
                    b2 = banks.pop(0)
                    DMA("sp", cost[:, :], cos_src, (), K(80, 2), "cost")
                    DMA("sp", sint[:, :], sin_src, (), K(82, 2), "sint")
                    TT("dve", SLf(S_TMP0, 0, 64), ps[b1][0:64, :], cost[:, :], ALU.mult, PK(b1) + K(80, 2), K(S_TMP0, 2))
                    TT("dve", SLf(S_SG0, 0, 64), ps[b2][0:64, :], sint[:, :], ALU.mult, PK(b2) + K(82, 2), K(S_SG0, 2))
                    TT("dve", krT[0:64, koff:koff + T], SLf(S_TMP0, 0, 64), SLf(S_SG0, 0, 64), ALU.add, K(S_TMP0, 2) + K(S_SG0, 2), [("krT", koff)])
                else:
                    CP("act", SLf(S_KRF, 0, 64), ps[b1][0:64, :], PK(b1), K(S_KRF, 2))
                    CP("dve", krTp[0:64, :], SLf(S_KRF, 0, 64), K(S_KRF, 2), ["krTp"])
                    ost = arena[:, S_OST * 512:(S_OST + 4) * 512].bitcast(F32).rearrange("p (t f) -> p t f", t=4)
                    for t2 in range(2):
                        b = nb()
                        for ti in range(2):
                            tt = t2 * 2 + ti
                            for c in range(2):
                                TR(ps[b][:, (ti * 2 + c) * 128:(ti * 2 + c + 1) * 128], SLf(S_CKR + 2 * c)[:, tt * 128:(tt + 1) * 128], ident[:],
                                   K(S_CKR + 2 * c, 2) + ["ident"], PK(b))
                        CP("act", ost[:, t2 * 2:t2 * 2 + 2, :], ps[b][:, :].rearrange("p (t f) -> p t f", t=2), PK(b), K(S_OST, 4))
                    DMA("sp", D["nckv"][g * T:(g + 1) * T, :].rearrange("(t p) f -> p t f", p=128), ost, K(S_OST, 4), (), "ost")
                    ost2 = arena[:, S_OST2 * 512:(S_OST2 + 1) * 512].bitcast(F32).rearrange("p (t f) -> p t f", t=4)
                    b = nb()
                    for tt in range(4):
                        TR(ps[b][:, tt * 64:(tt + 1) * 64], SLf(S_KRF, 0, 64)[:, tt * 128:(tt + 1) * 128], ident[0:64, 0:64],
                           K(S_KRF, 2) + ["ident"], PK(b))
                    CP("dve", ost2, ps[b][:, 0:256].rearrange("p (t f) -> p t f", t=4), PK(b), K(S_OST2))
                    DMA("sp", D["nkr"][g * T:(g + 1) * T, :].rearrange("(t p) f -> p t f", p=128), ost2, K(S_OST2), (), "ost2")

        def q_proj(sample, cos_src=None, sin_src=None):
            wt, wk = WLOAD(D["wq"][:, :])
            wv_ = wt[:].rearrange("p (k c) -> p k c", k=2)
            qlk = K(S_QLN, 2)
            qln = lambda kc: SL(S_QLN + kc)
            MEMSET("pool", SL(S_QR, 8, 64, 128), 0.0, K(S_QR, 8) + ["qrz"])
            if sample:
                DMA("sp", cost[:, :], cos_src, (), K(80, 2), "cost")
                DMA("sp", sint[:, :], sin_src, (), K(82, 2), "sint")
            for h in range(8):
                b = lin_chunk(wv_, wk, h * 128, 128, 2, qln, qlk)
                CP("act" if h % 2 == 0 else "dve", SL(S_QN + h), ps[b][:, :], PK(b), K(S_QN + h))
            for h in range(8):
                b1 = lin_chunk(wv_, wk, 1024 + h * 64, 128, 2, qln, qlk)
                if sample:
                    b2 = lin_chunk(wv_, wk, 1536 + h * 64, 128 if h < 7 else 64, 2, qln, qlk)
                    TT("dve", SLf(S_TMP0, 0, 64), ps[b1][0:64, :], cost[:, :], ALU.mult, PK(b1) + K(80, 2), K(S_TMP0, 2))
                    TT("dve", SLf(S_SG0, 0, 64), ps[b2][0:64, :], sint[:, :], ALU.mult, PK(b2) + K(82, 2), K(S_SG0, 2))
                    TT("dve", SL(S_QR + h, 1, 0, 64), SLf(S_TMP0, 0, 64), SLf(S_SG0, 0, 64), ALU.add, K(S_TMP0, 2) + K(S_SG0, 2), K(S_QR + h))
                else:
                    CP("act", SL(S_QR + h, 1, 0, 64), ps[b1][0:64, :], PK(b1), K(S_QR + h))

        wkvv = wkv[:].rearrange("p (k c) -> p k c", k=2)

        NKV = NKT * 128

        def attn_prompt():
            knT = kvb1[:, 0:8 * T].rearrange("p (h t) -> p h t", h=8)
            Vp = kvb1[:, 4096:4096 + 4 * 8 * 128].rearrange("p (t h c) -> p t h c", t=4, h=8)
            ckn = lambda kc: SL(S_CKN + kc)
            ckk = K(S_CKN, 2)
            for h in range(8):
                b = lin_chunk(wkvv, ["wkv"], h * 128, 128, 2, ckn, ckk)
                CP("act" if h % 2 == 0 else "dve", knT[:, h, :], ps[b][:, :], PK(b), [("knT", h)])
            for tt in range(4):
                for hh in range(2):
                    b = nb()
                    for kc in range(2):
                        MM(ps[b][:, :], SL(S_CKN + kc)[:, tt * 128:(tt + 1) * 128], wkvv[:, kc, 1024 + hh * 512:1024 + (hh + 1) * 512], kc == 0, kc == 1,
                           ckk + ["wkv"], PK(b))
                    CP("act" if hh == 0 else "dve", Vp[:, tt, hh * 4:(hh + 1) * 4, :], ps[b][:, :].rearrange("p (h c) -> p h c", h=4), PK(b), [("Vp", tt, hh)])
            pend = []

            def fin(item):
                it, sq, h, pt = item
                ptv = SL(pt).rearrange("p (k q) -> p k q", k=2)
                ob = nb()
                for kt in range(2):
                    MM(ps[ob][:, 0:256], Vp[:, 2 * sq + kt, h, :], ptv[:, kt, :], kt == 0, kt == 1, K(pt) + [("Vp", 2 * sq + kt, h // 4)], PK(ob))
                for kt in range(2):
                    MM(ps[ob][:, 256:512], onesb[:], ptv[:, kt, :], kt == 0, kt == 1, K(pt) + ["onesb"], PK(ob))
                tk = S_TMP0 if it % 2 == 0 else S_TMP1
                RCP(SLf(tk)[:, 0:256], ps[ob][:, 256:512], PK(ob), K(tk, 2))
                TT("dve", SL(S_OUTB + h)[:, sq * 256:(sq + 1) * 256], ps[ob][:, 0:256], SLf(tk)[:, 0:256], ALU.mult, PK(ob) + K(tk, 2), K(S_OUTB + h))

            it = 0
            for sq in range(2):
                for h in range(8):
                    sbk = nb()
                    sv = ps[sbk][:, :].rearrange("p (k q) -> p k q", k=2)
                    for kt in range(2):
                        tok0 = (2 * sq + kt) * 128
                        MM(sv[:, kt, :], knT[:, h, tok0:tok0 + 128], SL(S_QN + h)[:, sq * 256:(sq + 1) * 256], True, False,
                           [("knT", h)] + K(S_QN + h), PK(sbk))
                        MM(sv[:, kt, :], krTp[:, tok0:tok0 + 128], SL(S_QR + h)[:, sq * 256:(sq + 1) * 256], False, True,
                           ["krTp", "krTpz", "qrz"] + K(S_QR + h), PK(sbk))
                    pt = S_PT + (it % 4)
                    ACT(SL(pt), ps[sbk][:, :], AF.Exp, PK(sbk), K(pt), scale=SCALE)
                    pend.append((it, sq, h, pt))
                    if len(pend) > 2:
                        fin(pend.pop(0))
                    it += 1
            while pend:
                fin(pend.pop(0))

        KB = [kvb1[:, 0:NK], kK2[:, 0:NK]]

        NKB = (NK + 511) // 512

        def k_gen(h, par, only=None):
            KhT = KB[par]
            for kb in (range(NKB) if only is None else [only]):
                w = min(512, NK - kb * 512)
                b = nb()
                for kc in range(2):
                    MM(ps[b][:, 0:w], wkvv[:, kc, h * 128:(h + 1) * 128], ckvT[:, kc * NK + kb * 512:kc * NK + kb * 512 + w], kc == 0, kc == 1,
                       ["wkv", "ckvT_all"], PK(b))
                CP("act" if kb % 2 == 0 else "dve", KhT[:, kb * 512:kb * 512 + w], ps[b][:, 0:w], PK(b), [("kK", par)] + ([("kvb", 0)] if par == 0 else []))

        def v_gen(h):
            Vh = kvb1[:, NK:NK + NKV].rearrange("p (k c) -> p k c", c=128)
            for k0 in range(0, NKT, 4):
                n = min(4, NKT - k0)
                b = nb()
                for j in range(n):
                    kt = k0 + j
                    for kc in range(2):
                        MM(ps[b][:, j * 128:(j + 1) * 128], ckvT[:, kc * NK + kt * 128:kc * NK + (kt + 1) * 128], wkvv[:, kc, 1024 + h * 128:1024 + (h + 1) * 128],
                           kc == 0, kc == 1, ["wkv", "ckvT_all"], PK(b))
                CP("act", Vh[:, k0:k0 + n, :], ps[b][:, 0:n * 128].rearrange("p (k c) -> p k c", c=128), PK(b), [("kV",), ("kvb", 0)])

        def attn_pre():
            k_gen(0, 0)
            v_gen(0)

        def attn_sample():
            for b_ in (4, 5, 6, 7):
                reserved.add(b_)
            it = 0
            PT_RING = [52, 53, 54, 55, 56, 57, 60, 61, 62, 63]
            accD, accP = S_SG0, S_SG1
            for h in range(8):
                hp = h % 2
                KhT = KB[hp]
                Vh = kvb1[:, NK:NK + NKV].rearrange("p (k c) -> p k c", c=128)
                ot, dn = 4 + hp, 6 + hp
                pend = []
                nD = [0]
                nP = [0]
                kstep = max(1, NKT // NKB)
                knext = [0]

                def fin(item):
                    kt, pt = item
                    MM(ps[ot][:, :], Vh[:, kt, :], SL(pt), kt == 0, kt == NKT - 1, K(pt) + [("kV",)], PK(ot))
                    if kt % 2 == 1:
                        if nP[0] == 0:
                            CP("pool", SLf(accP), SL(pt), K(pt), K(accP, 2))
                        else:
                            TT("pool", SLf(accP), SL(pt), SLf(accP), ALU.add, K(pt) + K(accP, 2), K(accP, 2))
                        nP[0] += 1
                    else:
                        if nD[0] == 0:
                            CP("dve", SLf(accD), SL(pt), K(pt), K(accD, 2))
                        else:
                            TT("dve", SLf(accD), SL(pt), SLf(accD), ALU.add, K(pt) + K(accD, 2), K(accD, 2))
                        nD[0] += 1

                for kt in range(NKT):
                    sbk = nb()
                    MM(ps[sbk][:, :], KhT[:, kt * 128:(kt + 1) * 128], SL(S_QN + h), True, False, [("kK", hp)] + K(S_QN + h), PK(sbk))
                    MM(ps[sbk][:, :], krT[:, kt * 128:(kt + 1) * 128], SL(S_QR + h), False, True,
                       ["krT_all", "krTz", "qrz"] + K(S_QR + h), PK(sbk))
                    pt = PT_RING[it % len(PT_RING)]
                    it += 1
                    ACT(SL(pt), ps[sbk][:, :], AF.Exp, PK(sbk), K(pt), scale=SCALE)
                    pend.append((kt, pt))
                    if len(pend) > 2:
                        fin(pend.pop(0))
                    if h < 7 and kt % kstep == 1 and knext[0] < NKB:
                        k_gen(h + 1, 1 - hp, only=knext[0])
                        knext[0] += 1
                while pend:
                    fin(pend.pop(0))
                while h < 7 and knext[0] < NKB:
                    k_gen(h + 1, 1 - hp, only=knext[0])
                    knext[0] += 1
                tk = S_TMP0 if hp == 0 else S_TMP1
                if nP[0] > 0:
                    TT("dve", SLf(accD), SLf(accD), SLf(accP), ALU.add, K(accD, 2) + K(accP, 2), K(accD, 2))
                hi, lo = S_PT + 6, S_PT + 7
                CP("dve", SL(hi), SLf(accD), K(accD, 2), K(hi))
                TT("dve", SL(lo), SLf(accD), SL(hi), ALU.subtract, K(accD, 2) + K(hi), K(lo))
                if h < 7:
                    v_gen(h + 1)
                MM(ps[dn][:, :], onesb[:], SL(hi), True, False, K(hi) + ["onesb"], PK(dn))
                MM(ps[dn][:, :], onesb[:], SL(lo), False, True, K(lo) + ["onesb"], PK(dn))
                RCP(SLf(tk), ps[dn][:, :], PK(dn), K(tk, 2))
                TT("dve", SL(S_OUTB + h), ps[ot][:, :], SLf(tk), ALU.mult, PK(ot) + K(tk, 2), K(S_OUTB + h))
            for b_ in (4, 5, 6, 7):
                reserved.discard(b_)

        def gmlp_prefetch():
            return [WLOAD(D["wu"][i]) for i in range(2)], [WLOAD(D["wv"][i]) for i in range(2)]

        def gmlp_merge_out(cd, pre=None):
            hk = K(S_H, 8)
            wts = pre[0] if pre is not None else [WLOAD(D["wu"][i]) for i in range(2)]
            for g in range(8):
                wt, wk = wts[g // 4]
                wv_ = wt[:].rearrange("p (k c) -> p k c", k=8)
                b = lin_chunk(wv_, wk, (g % 4) * 128, 128, 8, hs, hk)
                CP("act", SL(S_U + g), ps[b][:, :], PK(b), K(S_U + g))
            wts = pre[1] if pre is not None else [WLOAD(D["wv"][i]) for i in range(2)]
            for tt in range(4):
                bb = []
                for hh in range(2):
                    wt, wk = wts[hh]
                    wv_ = wt[:].rearrange("p (k c) -> p k c", k=8)
                    b = nb()
                    for kc in range(8):
                        MM(ps[b][:, :], hs(kc)[:, tt * 128:(tt + 1) * 128], wv_[:, kc, :], kc == 0, kc == 7, wk + hk, PK(b))
                    ACT(SL(S_SQ + hh), ps[b][:, :], AF.Square, PK(b), K(S_SQ + hh) + [("small", 2 + hh)], accum=small[:, 8 + hh:9 + hh])
                    bb.append(b)
                TT("dve", small[:, 10:11], small[:, 8:9], small[:, 9:10], ALU.add, [("small", 2), ("small", 3)], [("small", 4)])
                ACT(small[:, 11:12], small[:, 10:11], AF.Sqrt, [("small", 4), "epst"], [("small", 5)], scale=1.0 / 1024, bias=epst[:, 0:1])
                RCP(small[:, 12:13], small[:, 11:12], [("small", 5)], [("small", 6)])
                for hh in range(2):
                    STT(SL(S_VV + tt * 2 + hh), ps[bb[hh]][:, :], small[:, 12:13], gvbc[:, hh * 512:(hh + 1) * 512], ALU.mult, ALU.mult,
                        PK(bb[hh]) + [("small", 6), "gvbc"], K(S_VV + tt * 2 + hh))
            wsv = wsT[:].rearrange("p (g q) -> p g q", g=8)
            for g in range(8):
                b = nb()
                for tt in range(4):
                    vsl = S_VV + tt * 2 + g // 4
                    MM(ps[b][:, tt * 128:(tt + 1) * 128], SL(vsl)[:, (g % 4) * 128:(g % 4 + 1) * 128], wsv[:, g, :], True, True,
                       K(vsl) + ["wsT"], PK(b))
                tk = S_TMP0 if g % 2 == 0 else S_TMP1
                TT("dve", SLf(tk).rearrange("p (t q) -> p t q", t=4), ps[b][:, :].rearrange("p (t q) -> p t q", t=4),
                   bsbc[:, g * 128:(g + 1) * 128].unsqueeze(1).to_broadcast([128, 4, 128]), ALU.add, PK(b) + ["bsbc"], K(tk, 2))
                TT("dve", SL(S_U + g), SLf(tk), SL(S_U + g), ALU.mult, K(tk, 2) + K(S_U + g), K(S_U + g))
            for bi, br in enumerate((1, 0)):
                wsrc = D["wa"] if br == 0 else D["wb"]
                insl = S_U if br == 0 else S_OUTB
                for half in range(2):
                    wtp, wkp = WLOAD(wsrc[half])
                    wtg, wkg = WLOAD(D["wg"][br * 2 + half])
                    wvp = wtp[:].rearrange("p (k c) -> p k c", k=8)
                    wvg = wtg[:].rearrange("p (k c) -> p k c", k=8)
                    for mm_ in range(4):
                        m = half * 4 + mm_
                        bt = lin_chunk(wvp, wkp, mm_ * 128, 128, 8, lambda kc: SL(insl + kc), K(insl, 8))
                        bl = lin_chunk(wvg, wkg, mm_ * 128, 128, 8, hs, hk)
                        sg = S_SG0 if m % 2 == 0 else S_SG1
                        ACT(SLf(sg), ps[bl][:, :], AF.Sigmoid, PK(bl), K(sg, 2))
                        if bi == 0:
                            TT("dve", SL(S_MRG + m), SLf(sg), ps[bt][:, :], ALU.mult, K(sg, 2) + PK(bt), K(S_MRG + m))
                        else:
                            tk = S_TMP0 if m % 2 == 0 else S_TMP1
                            TT("dve", SLf(tk), SLf(sg), ps[bt][:, :], ALU.mult, K(sg, 2) + PK(bt), K(tk, 2))
                            TT("dve", SL(S_MRG + m), SLf(tk), SL(S_MRG + m), ALU.add, K(tk, 2) + K(S_MRG + m), K(S_MRG + m))
            ACT(small[:, 30:31], epst[:, 0:1], AF.Sqrt, ["epst"], [("small", 30)])
            sb_ = stat_begin()
            pend = None
            for half in range(2):
                wt, wk = WLOAD(D["wo"][half])
                wv_ = wt[:].rearrange("p (k c) -> p k c", k=8)
                for mm_ in range(4):
                    m = half * 4 + mm_
                    b = lin_chunk(wv_, wk, mm_ * 128, 128, 8, lambda kc: SL(S_MRG + kc), K(S_MRG, 8))
                    if pend is not None:
                        stat_mm(pend)
                    STT(xs(m), ps[b][:, :], dcs(cd, 1, 2, m), xs(m), ALU.mult, ALU.add, PK(b) + xk(m) + ["dc"], xk(m))
                    pend = stat_chunk(sb_, m, m == 0, m == 7)
            stat_mm(pend)

        def load_gf():
            DMA("sp", arena[:, 48 * 512:52 * 512].bitcast(F32), D["gfbc"][:, :], (), K(48, 4), "gfbc")

        def final_store(dst_rows):
            gfv = arena[:, 48 * 512:52 * 512].bitcast(F32)
            for tt in range(4):
                pc = 16 + 5 * (tt % 2)
                so = S_XIN + 4 * (tt % 2)
                stg = arena[:, so * 512:(so + 4) * 512].bitcast(F32)
                bb = []
                for half in range(2):
                    b = nb()
                    for k in range(4):
                        m = half * 4 + k
                        TR(ps[b][:, k * 128:(k + 1) * 128], xs(m)[:, tt * 128:(tt + 1) * 128], ident[:], xk(m) + ["ident"], PK(b))
                    ACT(SL(S_SQ + 4 * (tt % 2) + half), ps[b][:, :], AF.Square, PK(b), K(S_SQ + 4 * (tt % 2) + half) + [("small", pc + half)],
                        accum=small[:, pc + half:pc + half + 1])
                    bb.append(b)
                TT("dve", small[:, pc + 2:pc + 3], small[:, pc:pc + 1], small[:, pc + 1:pc + 2], ALU.add, [("small", pc), ("small", pc + 1)], [("small", pc + 2)])
                ACT(small[:, pc + 3:pc + 4], small[:, pc + 2:pc + 3], AF.Sqrt, [("small", pc + 2), "epst"], [("small", pc + 3)], scale=1.0 / 1024, bias=epst[:, 0:1])
                RCP(small[:, pc + 4:pc + 5], small[:, pc + 3:pc + 4], [("small", pc + 3)], [("small", pc + 4)])
                for half in range(2):
                    STT(stg[:, half * 512:(half + 1) * 512], ps[bb[half]][:, :], small[:, pc + 4:pc + 5], gfv[:, half * 512:(half + 1) * 512], ALU.mult, ALU.mult,
                        PK(bb[half]) + [("small", pc + 4)] + K(48, 4), K(so + 2 * half, 2))
                DMA("sp", dst_rows[tt * 128:(tt + 1) * 128, :], stg, K(so, 4), (), ("xin", tt % 2))

        def reload_x(i):
            DMA("sp", xall(), xs1[i], [("xs1", i)], xallk(), ("xs1", i))
            sb_ = stat_begin()
            pend = None
            for m in range(8):
                it_ = stat_chunk(sb_, m, m == 0, m == 7)
                if pend is not None:
                    stat_mm(pend)
                pend = it_
            stat_mm(pend)

        groups = []
        for g in range(NPH):
            groups.append(dict(kind="P", idx=g, cd=0, n0=0, load=(lambda g=g: load_x(D["xp"][g * T:(g + 1) * T, :]))))
        for i in range(NOH + NOX):
            own = i < NOH
            ii = i if own else i - NOH
            src = D["xso"] if own else D["xsx"]
            groups.append(dict(kind="A", idx=i, cd=1, n0=0, load=(lambda src=src, ii=ii: load_x(src[ii * T:(ii + 1) * T, :]))))
        for i in range(NOH):
            groups.append(dict(kind="B", idx=i, cd=1, n0=1, load=(lambda i=i: reload_x(i))))

        USE_EARLY = False

        def prefetch(gi, with_norm):
            if gi >= len(groups):
                return
            grp = groups[gi]
            if grp["kind"] == "B" and groups[gi - 1]["kind"] == "A":
                return
            save = par[0]
            par[0] = gi % 2
            if not grp.get("loaded"):
                grp["load"]()
                grp["loaded"] = True
                if not with_norm and USE_EARLY:
                    rstd_from_pending(early=True)
            if with_norm and not grp.get("normed"):
                norm_mod(grp["cd"], grp["n0"])
                grp["normed"] = True
            par[0] = save

        did_ctx = False
        did_barrier = False
        for gi, grp in enumerate(groups):
            par[0] = gi % 2
            kind, cd = grp["kind"], grp["cd"]
            if kind == "A" and not did_ctx:
                did_ctx = True
                pk = [("knT", h) for h in range(8)] + [("Vp", tt, hh) for tt in range(4) for hh in range(2)]
                MEMSET("dve", small[:, 13:14], 0.0, pk + [("kvb", 0)])
                for kt in range(PAST // 128):
                    so = S_XIN + 4 * (kt % 2)
                    stg = arena[:, so * 512:(so + 4) * 512].bitcast(F32)
                    DMA("sp", stg[:, 0:256], D["cck"][kt * 128:(kt + 1) * 128, :], (), K(so, 2), ("xin", kt % 2))
                    DMA("sp", stg[:, 512:576], D["ckr"][kt * 128:(kt + 1) * 128, :], (), K(so + 2, 1), ("xinb", kt % 2))
                    b = nb()
                    for c in range(2):
                        TR(ps[b][:, c * 128:(c + 1) * 128], stg[:, c * 128:(c + 1) * 128], ident[:], K(so, 2) + ["ident"], PK(b))
                    TR(ps[b][0:64, 256:384], stg[:, 512:576], ident[:], K(so + 2, 1) + ["ident"], PK(b))
                    for c in range(2):
                        CP("act", ckvT[:, c * NK + kt * 128:c * NK + (kt + 1) * 128], ps[b][:, c * 128:(c + 1) * 128], PK(b), [("ckvT", -1 - kt)])
                    CP("act", krT[0:64, kt * 128:(kt + 1) * 128], ps[b][0:64, 256:384], PK(b), [("krT", -1 - kt)])
            if kind == "B" and not did_barrier:
                did_barrier = True
                allk = [("ckvT", PAST + i * T) for i in range(NOH + NOX)] + [("ckvT", -1 - kt) for kt in range(PAST // 128)]
                allr = [("krT", PAST + i * T) for i in range(NOH + NOX)] + [("krT", -1 - kt) for kt in range(PAST // 128)]
                P.add("dve", lambda e: e.memset(small[:, 15:16], 0.0), allk, ["ckvT_all"])
                P.add("dve", lambda e: e.memset(small[:, 14:15], 0.0), allr, ["krT_all"])
            if not grp.get("loaded"):
                grp["load"]()
            normed = grp.get("normed", False)
            if kind == "P":
                g = grp["idx"]
                ffn(0, 0, D["f1w1"], D["f1w2"], skip_norm=normed)
                norm_mod(0, 1)
                qk_latent(0, False, g=g)
                q_proj(False)
                attn_prompt()
                gmlp_merge_out(0)
                load_gf()
                ffn(0, 2, D["f2w1"], D["f2w2"], stats_after=False, mid_hook=(lambda gi=gi: prefetch(gi + 1, True)))
                final_store(D["yp"][g * T:(g + 1) * T, :])
            elif kind == "A":
                i = grp["idx"]
                own = i < NOH
                ii = i if own else i - NOH
                ffn(1, 0, D["f1w1"], D["f1w2"], skip_norm=normed, mid_hook=(lambda gi=gi: prefetch(gi + 1, False)))
                if own:
                    DMA("sp", xs1[ii], xall(), xallk(), [("xs1", ii)], ("xs1", ii))
                norm_mod(1, 1)
                koff = PAST + i * T
                cs = (D["cos_o"] if own else D["cos_x"])[:, ii * T:(ii + 1) * T]
                sn = (D["sin_o"] if own else D["sin_x"])[:, ii * T:(ii + 1) * T]
                qk_latent(1, True, koff=koff, want_q=False, cos_src=cs, sin_src=sn, mid_hook=(lambda gi=gi: prefetch(gi + 1, True)))
            else:
                i = grp["idx"]
                if not normed:
                    norm_mod(1, 1)
                qk_latent(1, True, want_q=True, want_kv=False)
                attn_pre()
                q_proj(True, D["cos_o"][:, i * T:(i + 1) * T], D["sin_o"][:, i * T:(i + 1) * T])
                pre = gmlp_prefetch()
                attn_sample()
                gmlp_merge_out(1, pre)
                load_gf()
                ffn(1, 2, D["f2w1"], D["f2w2"], stats_after=False, mid_hook=(lambda gi=gi: prefetch(gi + 1, True)))
                final_store(D["ys"][i * T:(i + 1) * T, :])
        P.emit(st)
    return nc


def _kmaj(W):
    Kd, N = W.shape
    return np.ascontiguousarray(W.reshape(Kd // 128, 128, N).transpose(1, 0, 2)).reshape(128, (Kd // 128) * N)


def _blocks(W, bc):
    Kd, N = W.shape
    KC = Kd // 128
    return np.ascontiguousarray(W.reshape(KC, 128, N // bc, bc).transpose(2, 1, 0, 3)).reshape(N // bc, 128, KC * bc)


def _pp(v):
    return np.ascontiguousarray(v.reshape(-1, 128).T)


_SWAP = np.concatenate([np.arange(16, 32), np.arange(0, 16), np.arange(48, 64), np.arange(32, 48)])


def prep_shared(inp):
    f = lambda a: np.asarray(a, dtype=np.float32)
    sh = {}
    sh["modw"] = _blocks(f(inp["mod_w"])[0], 512)
    sh["modb"] = _pp(f(inp["mod_b"])[0])
    sh["gains"] = np.concatenate([_pp(f(inp["norm_ffn1"])[0]), _pp(f(inp["norm_mix"])[0]), _pp(f(inp["norm_ffn2"])[0]), _pp(f(inp["norm_final"]))], axis=1)
    sh["gains2"] = np.concatenate([_pp(f(inp["q_norm"])[0]), _pp(f(inp["kv_norm"])[0])], axis=1)
    sh["gvbc"] = np.ascontiguousarray(np.broadcast_to(f(inp["gmlp_v_norm"])[0][None, :], (128, 1024)))
    sh["gfbc"] = np.ascontiguousarray(np.broadcast_to(f(inp["norm_final"])[None, :], (128, 1024)))
    bs = f(inp["gmlp_b_s"])[0]
    sh["bsbc"] = np.ascontiguousarray(np.broadcast_to(bs.T.reshape(1, 1024), (128, 1024)))
    ws = f(inp["gmlp_w_s"])[0]
    sh["wsT"] = np.ascontiguousarray(ws.transpose(2, 0, 1)).reshape(128, 1024)
    sh["ident"] = np.eye(128, dtype=np.float32)
    for nm, key_in, key_out in (("f1", "ffn1_w_in", "ffn1_w_out"), ("f2", "ffn2_w_in", "ffn2_w_out")):
        W1 = f(inp[key_in])[0]
        sh[nm + "w1"] = np.ascontiguousarray(W1.reshape(8, 128, 2, 11, 256).transpose(3, 1, 0, 2, 4)).reshape(11, 128, 4096)
        W2 = f(inp[key_out])[0]
        sh[nm + "w2"] = np.ascontiguousarray(W2.reshape(22, 128, 8, 128).transpose(2, 1, 0, 3)).reshape(8, 128, 2816)
    Wi = f(inp["w_in"])[0]
    sh["wu"] = _blocks(Wi[:, 0:1024], 512)
    sh["wv"] = _blocks(Wi[:, 1024:2048], 512)
    sh["wql"] = _kmaj(Wi[:, 2048:2304])
    sh["wck"] = _kmaj(Wi[:, 2304:2560])
    kr = Wi[:, 2560:2624]
    sh["wkr"] = _kmaj(np.concatenate([kr, kr[:, _SWAP], kr], axis=1))
    sh["wg"] = _blocks(Wi[:, 2624:4672], 512)
    Wq = f(inp["w_q_up"])[0].reshape(256, 8, 192)
    qn = Wq[:, :, 0:128].reshape(256, 1024)
    qr = Wq[:, :, 128:192]
    sh["wq"] = _kmaj(np.concatenate([qn, qr.reshape(256, 512), qr[:, :, _SWAP].reshape(256, 512)], axis=1))
    Wkv = f(inp["w_kv_up"])[0].reshape(256, 8, 256)
    sh["wkv"] = _kmaj(np.concatenate([Wkv[:, :, 0:128].reshape(256, 1024), Wkv[:, :, 128:256].reshape(256, 1024)], axis=1))
    sh["wa"] = _blocks(f(inp["w_a_proj"])[0], 512)
    sh["wb"] = _blocks(f(inp["w_b_proj"])[0], 512)
    sh["wo"] = _blocks(f(inp["w_o"])[0], 512)
    return sh


def rope_tables(pos):
    pos = np.asarray(pos)
    r = (pos // 64).astype(np.float32)
    c = (pos % 64).astype(np.float32)
    inv = (1.0 / (np.float32(10000.0) ** (np.arange(0, 32, 2, dtype=np.float32) / np.float32(32)))).astype(np.float32)
    ang_r = r[None, :] * inv[:, None]
    ang_c = c[None, :] * inv[:, None]
    cos = np.concatenate([np.cos(ang_r), np.cos(ang_r), np.cos(ang_c), np.cos(ang_c)], axis=0)
    sin = np.concatenate([-np.sin(ang_r), np.sin(ang_r), -np.sin(ang_c), np.sin(ang_c)], axis=0)
    return np.ascontiguousarray(cos.astype(np.float32)), np.ascontiguousarray(sin.astype(np.float32))


def run_cfg(inp, n_cores, NPH, L):
    f = lambda a: np.asarray(a, dtype=np.float32)
    halfL = L // 2
    NOH = halfL // T
    NOX = NOH
    sh = prep_shared(inp)
    xp = f(inp["x_prompt"]).reshape(-1, 1024)
    xsm = f(inp["x_sample"])
    cc = f(inp["c"])
    cctx = f(inp["c_ctx"])
    cck = f(inp["cache_ckv"])
    ckr = f(inp["cache_krope"])
    in_maps = []
    for core in range(n_cores):
        s, hf = core // 2, core % 2
        m = dict(sh)
        m["xp"] = np.ascontiguousarray(xp[core * NPH * T:(core + 1) * NPH * T])
        own = np.arange(hf * halfL, (hf + 1) * halfL)
        oth = np.arange((1 - hf) * halfL, (2 - hf) * halfL)
        m["xso"] = np.ascontiguousarray(xsm[s, own])
        m["xsx"] = np.ascontiguousarray(xsm[s, oth])
        m["cv"] = np.ascontiguousarray(np.stack([_pp(cctx), _pp(cc[s])], axis=2)).reshape(128, 16)
        m["cck"] = np.ascontiguousarray(cck[s, 0])
        m["ckr"] = np.ascontiguousarray(ckr[s, 0])
        m["cos_o"], m["sin_o"] = rope_tables(own)
        m["cos_x"], m["sin_x"] = rope_tables(oth)
        in_maps.append(m)
    nc = build_program(NPH, NOH, NOX)
    res = run_bass_kernel_spmd(nc, in_maps, core_ids=list(range(n_cores)))
    r = res.results
    yp = np.concatenate([r[c]["yp"] for c in range(n_cores)], axis=0)
    nckv = np.concatenate([r[c]["nckv"] for c in range(n_cores)], axis=0)
    nkr = np.concatenate([r[c]["nkr"] for c in range(n_cores)], axis=0)
    ys = np.stack([np.concatenate([r[2 * s]["ys"], r[2 * s + 1]["ys"]], axis=0) for s in range(n_cores // 2)], axis=0)
    return yp, ys, nckv, nkr


def kernel(**inputs):
    B, S = inputs["x_prompt"].shape[0], inputs["x_prompt"].shape[1]
    yp, ys, nckv, nkr = run_cfg(inputs, 8, (B * S) // (8 * T), inputs["x_sample"].shape[1])
    return (yp.reshape(B, S, 1024).astype(np.float32), ys.astype(np.float32),
            nckv.reshape(B, 1, S, 256).astype(np.float32), nkr.reshape(B, 1, S, 64).astype(np.float32))
```

```python
import numpy as np
from contextlib import ExitStack
import concourse.bass as bass
import concourse.mybir as mybir
from concourse.bass_utils import run_bass_kernel_spmd

F32 = mybir.dt.float32
BF16 = mybir.dt.bfloat16
AF = mybir.ActivationFunctionType
ALU = mybir.AluOpType

ENGS = ("pe", "act", "dve", "pool", "sp")
EPS = 1e-6
T = 512
SCALE = 192.0 ** -0.5


class Prog:
    def __init__(self, nc, same_engine_sync=True):
        self.nc = nc
        self.ops = []
        self.last_write = {}
        self.readers = {}
        self.same_engine_sync = same_engine_sync
        self.group_keys = set()

    def add(self, eng, fn, reads=(), writes=(), dma=None, group=False):
        i = len(self.ops)
        deps = set()
        for r in reads:
            lw = self.last_write.get(r)
            if lw is not None:
                deps.add(lw)
        for w in writes:
            lw = self.last_write.get(w)
            if lw is not None:
                deps.add(lw)
            rd = self.readers.get(w)
            if rd:
                for v in rd.values():
                    if isinstance(v, list):
                        deps.update(v)
                    else:
                        deps.add(v)
        is_dma = dma is not None
        for r in reads:
            rd = self.readers.setdefault(r, {})
            if is_dma:
                rd.setdefault("_dma", []).append(i)
            else:
                rd[eng] = i
        for w in writes:
            self.last_write[w] = i
            self.readers[w] = {}
        deps.discard(i)
        if group:
            self.group_keys.add(dma)
        self.ops.append(dict(eng=eng, fn=fn, deps=deps, dma=dma))
        return i

    def emit(self, stack):
        nc = self.nc
        ops = self.ops
        dma_cnt = {}
        for op in ops:
            if op["dma"] is not None:
                k = op["dma"]
                dma_cnt[k] = dma_cnt.get(k, 0) + 16
                op["dma_val"] = dma_cnt[k]
        for op in ops:
            if op["dma"] is not None and op["dma"] in self.group_keys:
                op["dma_val"] = dma_cnt[op["dma"]]
        for i, op in enumerate(ops):
            eng_dep = {}
            dma_dep = {}
            for j in op["deps"]:
                d = ops[j]
                if d["dma"] is not None:
                    k = d["dma"]
                    dma_dep[k] = max(dma_dep.get(k, 0), d["dma_val"])
                else:
                    e2 = d["eng"]
                    if e2 == op["eng"] and op["dma"] is None:
                        if e2 == "pe" or not self.same_engine_sync:
                            continue
                    eng_dep[e2] = max(eng_dep.get(e2, -1), j)
            op["eng_dep"] = eng_dep
            op["dma_dep"] = dma_dep
            for e2, j in eng_dep.items():
                ops[j]["mark"] = True
        cnt = {e: 0 for e in ENGS}
        for op in ops:
            if op.get("mark"):
                cnt[op["eng"]] += 1
                op["val"] = cnt[op["eng"]]
        esem = {e: stack.enter_context(nc.semaphore("s_" + e)) for e in ENGS}
        dsem = {}
        for n, k in enumerate(dma_cnt):
            dsem[k] = stack.enter_context(nc.semaphore("d%d" % n))
        block = stack.enter_context(nc.Block())

        def run(eng):
            def body(e):
                waited = {}
                for op in ops:
                    if op["eng"] != eng:
                        continue
                    for e2, j in op["eng_dep"].items():
                        v = ops[j]["val"]
                        key = ("e", e2)
                        if waited.get(key, 0) < v:
                            e.wait_ge(esem[e2], v)
                            waited[key] = v
                    for k, v in op["dma_dep"].items():
                        key = ("d", k)
                        if waited.get(key, 0) < v:
                            e.wait_ge(dsem[k], v)
                            waited[key] = v
                    ins = op["fn"](e)
                    if op["dma"] is not None:
                        ins.then_inc(dsem[op["dma"]], 16)
                    elif op.get("mark"):
                        ins.then_inc(esem[eng], 1)
                if eng == "sp":
                    for k, v in dma_cnt.items():
                        e.wait_ge(dsem[k], v)
            return body

        block.sync(run("sp"))
        block.scalar(run("act"))
        block.vector(run("dve"))
        block.gpsimd(run("pool"))
        block.tensor(run("pe"))


S_X = 0
S_H = 16
S_A = 24
S_Y = 24
S_XIN = 40
S_QLR = 24
S_CKR = 28
S_U = 24
S_QLN = 32
S_CKN = 34
S_QN = 36
S_VV = 36
S_QR = 44
S_MRG = 44
S_SQ = 52
S_PT = 52
S_OB = 56
S_RSTD = 60
S_RT = 62
S_TMP0 = 64
S_TMP1 = 66
S_SG0 = 68
S_SG1 = 70
S_OUTB = 72
S_KRF = 80
S_OST = 82
S_OST2 = 86
S_X2 = 87
N_SLOTS = 103
NW = 4


def build_program(NPH, NOH, NOX, PAST=256):
    NK = PAST + (NOH + NOX) * T
    NKT = NK // 128
    nc = bass.Bass("TRN2", target_bir_lowering=False)
    D = {}

    def din(name, shape):
        D[name] = nc.dram_tensor(name, list(shape), F32, kind="ExternalInput").ap()
        return D[name]

    def dout(name, shape):
        D[name] = nc.dram_tensor(name, list(shape), F32, kind="ExternalOutput").ap()
        return D[name]

    din("modw", [18, 128, 4096]); din("modb", [128, 72]); din("gains", [128, 32]); din("gains2", [128, 4])
    din("gvbc", [128, 1024]); din("bsbc", [128, 1024]); din("gfbc", [128, 1024]); din("wsT", [128, 1024]); din("ident", [128, 128])
    din("f1w1", [11, 128, 4096]); din("f1w2", [8, 128, 2816]); din("f2w1", [11, 128, 4096]); din("f2w2", [8, 128, 2816])
    din("wu", [2, 128, 4096]); din("wv", [2, 128, 4096]); din("wql", [128, 2048]); din("wck", [128, 2048])
    din("wkr", [128, 1536]); din("wg", [4, 128, 4096]); din("wq", [128, 4096]); din("wkv", [128, 4096])
    din("wa", [2, 128, 4096]); din("wb", [2, 128, 4096]); din("wo", [2, 128, 4096])
    din("xp", [NPH * T, 1024]); din("xso", [NOH * T, 1024]); din("xsx", [NOX * T, 1024])
    din("cv", [128, 16]); din("cck", [PAST, 256]); din("ckr", [PAST, 64])
    din("cos_o", [64, NOH * T]); din("sin_o", [64, NOH * T]); din("cos_x", [64, NOX * T]); din("sin_x", [64, NOX * T])
    dout("yp", [NPH * T, 1024]); dout("ys", [NOH * T, 1024]); dout("nckv", [NPH * T, 256]); dout("nkr", [NPH * T, 64])
    xs1 = nc.dram_tensor("xs1", [NOH, 128, 4096], F32, kind="Internal").ap()

    with ExitStack() as st:
        def sb(name, shape, dt):
            return st.enter_context(nc.sbuf_tensor("sb_" + name, list(shape), dt))

        arena = sb("arena", [128, N_SLOTS * 512], BF16)
        wsl = [sb("wsl%d" % i, [128, 4096], BF16) for i in range(NW)]
        wkv = sb("wkv", [128, 4096], BF16)
        ident = sb("ident_sb", [128, 128], F32)
        identb = sb("identb", [128, 128], BF16)
        onesb = sb("onesb", [128, 128], BF16)
        epst = sb("epst", [128, 1], F32)
        modb = sb("modb_sb", [128, 72], F32)
        mods = sb("mods", [128, 144], F32)
        gains = sb("gains_sb", [128, 32], F32)
        gains2 = sb("gains2_sb", [128, 4], F32)
        dc = sb("dc", [128, 2 * 3 * 3 * 8], F32)
        cvt = sb("cvt", [128, 16], F32)
        scond = sb("scond", [128, 16], BF16)
        gvbc = sb("gvbc_sb", [128, 1024], F32)
        bsbc = sb("bsbc_sb", [128, 1024], F32)
        wsT = sb("wsT_sb", [128, 1024], BF16)
        kK2 = sb("kK2", [128, NK], BF16)
        small = sb("small", [128, 32], F32)
        ckvT = sb("ckvT", [128, 2 * NK], BF16)
        krT = sb("krT", [128, NK], BF16)
        krTp = sb("krTp", [128, 512], BF16)
        kvb1 = sb("kvb", [128, max(NK + NKT * 128, 8192)], BF16)
        kvb = [kvb1, kvb1]
        ps = [st.enter_context(nc.psum_tensor("ps%d" % i, [128, 512], F32)) for i in range(8)]

        P = Prog(nc)
        bank_rr = [0]
        reserved = set()

        def nb():
            while True:
                b = bank_rr[0] % 8
                bank_rr[0] += 1
                if b not in reserved:
                    return b

        w_rr = [0]

        class _V:
            def __init__(self, ap):
                self.ap = ap

            def __getitem__(self, idx):
                return self.ap[idx]

        cost = _V(arena[0:64, 80 * 512:82 * 512].bitcast(F32))
        sint = _V(arena[0:64, 82 * 512:84 * 512].bitcast(F32))

        def SL(i, n=1, p0=0, p1=128):
            return arena[p0:p1, i * 512:(i + n) * 512]

        def SLf(i, p0=0, p1=128):
            return arena[p0:p1, i * 512:(i + 2) * 512].bitcast(F32)

        def K(i, n=1):
            return [("s", i + k) for k in range(n)]

        def PK(b):
            return [("p", b)]

        def MM(out, lhsT, rhs, start, stop, rd, wr, skip=False):
            P.add("pe", lambda e: e.matmul(out, lhsT=lhsT, rhs=rhs, start=start, stop=stop, skip_group_check=skip), rd, wr)

        def TR(out, in_, idn, rd, wr):
            P.add("pe", lambda e: e.transpose(out=out, in_=in_, identity=idn), rd, wr)

        def ACT(out, in_, func, rd, wr, scale=None, bias=None, accum=None):
            kw = {}
            if scale is not None:
                kw["scale"] = scale
            if bias is not None:
                kw["bias"] = bias
            if accum is not None:
                kw["accum_out"] = accum
            P.add("act", lambda e: e.activation(out=out, in_=in_, func=func, **kw), rd, wr)

        def CP(eng, out, in_, rd, wr):
            if eng == "act":
                P.add("act", lambda e: e.copy(out=out, in_=in_), rd, wr)
            else:
                P.add(eng, lambda e: e.tensor_copy(out=out, in_=in_), rd, wr)

        def TT(eng, out, in0, in1, op, rd, wr):
            P.add(eng, lambda e: e.tensor_tensor(out=out, in0=in0, in1=in1, op=op), rd, wr)

        def STT(out, in0, scalar, in1, op0, op1, rd, wr, eng="dve"):
            P.add(eng, lambda e: e.scalar_tensor_tensor(out=out, in0=in0, scalar=scalar, in1=in1, op0=op0, op1=op1), rd, wr)

        def TS(out, in0, s1, s2, op0, op1, rd, wr, eng="dve"):
            if s2 is None:
                P.add(eng, lambda e: e.tensor_scalar(out=out, in0=in0, scalar1=s1, scalar2=None, op0=op0), rd, wr)
            else:
                P.add(eng, lambda e: e.tensor_scalar(out=out, in0=in0, scalar1=s1, scalar2=s2, op0=op0, op1=op1), rd, wr)

        def RCP(out, in_, rd, wr):
            P.add("dve", lambda e: e.reciprocal(out=out, in_=in_), rd, wr)

        def MEMSET(eng, ap, val, wr):
            P.add(eng, lambda e: e.memset(ap, val), (), wr)

        def DMA(q, out, in_, rd, wr, key, group=False):
            P.add(q, lambda e: e.dma_start(out=out, in_=in_), rd, wr, dma=key, group=group)

        def WLOAD(src, n=4096):
            s = w_rr[0] % NW
            w_rr[0] += 1
            DMA("pool", wsl[s][:, 0:n], src, (), [("w", s)], ("w", s))
            return wsl[s], [("w", s)]

        def const_load(tile_ap, src, key):
            DMA("sp", tile_ap, src, (), [key], "const", group=True)

        const_load(ident[:], D["ident"][:, :], "ident")
        const_load(modb[:], D["modb"][:, :], "modb")
        const_load(gains[:], D["gains"][:, :], "gains")
        const_load(gains2[:], D["gains2"][:, :], "gains2")
        const_load(cvt[:], D["cv"][:, :], "cvt")
        const_load(gvbc[:], D["gvbc"][:, :], "gvbc")
        const_load(bsbc[:], D["bsbc"][:, :], "bsbc")
        DMA("pool", wkv[:], D["wkv"][:, :], (), ["wkv"], "wkv")
        DMA("pool", wsT[:], D["wsT"][:, :], (), ["wsT"], "wsT")
        CP("dve", identb[:], ident[:], ["ident"], ["identb"])
        MEMSET("dve", onesb[:], 1.0, ["onesb"])
        MEMSET("dve", krT[64:128, :], 0.0, ["krTz"])
        MEMSET("dve", krTp[64:128, :], 0.0, ["krTpz"])
        MEMSET("dve", epst[:], EPS, ["epst"])
        ACT(scond[:], cvt[:], AF.Silu, ["cvt"], ["scond"])
        scv = scond[:].rearrange("p (k c) -> p k c", c=2)
        mb = nb()
        for blk in range(18):
            wt, wk = WLOAD(D["modw"][blk])
            wv_ = wt[:].rearrange("p (k c) -> p k c", k=8)
            for cc in range(4):
                q = blk * 4 + cc
                for kc in range(8):
                    MM(ps[mb][:, q * 2:q * 2 + 2], wv_[:, kc, cc * 128:(cc + 1) * 128], scv[:, kc, :], kc == 0, kc == 7,
                       wk + ["scond"], PK(mb))
        m3 = mods[:].rearrange("p (q c) -> p q c", c=2)
        TT("dve", m3, ps[mb][:, 0:144].rearrange("p (q c) -> p q c", c=2), modb[:].unsqueeze(2).to_broadcast([128, 72, 2]),
           ALU.add, PK(mb) + ["modb"], ["mods"])
        dcv = dc[:].rearrange("p (c n k m) -> p c n k m", c=2, n=3, k=3)
        for cd in range(2):
            for n in range(3):
                sh = m3[:, (3 * n) * 8:(3 * n) * 8 + 8, cd]
                sc = m3[:, (3 * n + 1) * 8:(3 * n + 1) * 8 + 8, cd]
                gt = m3[:, (3 * n + 2) * 8:(3 * n + 2) * 8 + 8, cd]
                STT(dcv[:, cd, n, 0, :], sc, 1.0, gains[:, n * 8:(n + 1) * 8], ALU.add, ALU.mult, ["mods", "gains"], ["dc"])
                CP("dve", dcv[:, cd, n, 1, :], sh, ["mods"], ["dc"])
                TS(dcv[:, cd, n, 2, :], gt, 0.5 if n != 1 else 1.0, None, ALU.mult, None, ["mods"], ["dc"])

        def dcs(cd, n, k, m):
            return dcv[:, cd, n, k, m:m + 1]

        XB = [S_X, S_X2]
        par = [0]

        def xs(m):
            return SLf(XB[par[0]] + 2 * m)

        def xk(m):
            return K(XB[par[0]] + 2 * m, 2)

        def xall():
            return arena[:, XB[par[0]] * 512:(XB[par[0]] + 16) * 512].bitcast(F32)

        def xallk():
            return K(XB[par[0]], 16)

        def hs(kc):
            return SL(S_H + kc)

        def load_x(src_rows):
            sb_ = stat_begin()
            x3 = xall().rearrange("p (m t) -> p m t", m=8)
            for tt in range(4):
                si = S_TMP0 + 4 * (tt % 2)
                stg_full = arena[:, si * 512:(si + 4) * 512].bitcast(F32)
                sk = K(si, 4)
                DMA("sp", stg_full, src_rows[tt * 128:(tt + 1) * 128, :], (), sk, ("xi", tt % 2))
                KLX = 9
                for half in range(2 if KLX >= 1 else 0):
                    b = nb()
                    for k in range(4):
                        m = half * 4 + k
                        TR(ps[b][:, k * 128:(k + 1) * 128], stg_full[:, m * 128:(m + 1) * 128], ident[:], sk + ["ident"], PK(b))
                    for k in range(4 if KLX >= 2 else 0):
                        m = half * 4 + k
                        if KLX == 2:
                            eng = "act"
                        elif KLX == 3:
                            eng = "dve"
                        else:
                            eng = "act" if half == 0 else "dve"
                        CP(eng, xs(m)[:, tt * 128:(tt + 1) * 128], ps[b][:, k * 128:(k + 1) * 128], PK(b), xk(m))
                if KLX >= 2:
                    sq3 = SL(S_SQ + 2 * tt, 2).rearrange("p (m t) -> p m t", m=8)
                    ACT(sq3, x3[:, :, tt * 128:(tt + 1) * 128], AF.Square, xallk(), K(S_SQ + 2 * tt, 2))
                    for m in range(8):
                        MM(ps[sb_][:, tt * 128:(tt + 1) * 128], onesb[:], sq3[:, m, :], m == 0, m == 7, K(S_SQ + 2 * tt, 2) + ["onesb"], PK(sb_))

        def stats(sq_aps, sq_keys, nfeat, out_rstd, out_key):
            b = nb()
            n = len(sq_aps)
            for i, a in enumerate(sq_aps):
                MM(ps[b][:, :], onesb[:], a, i == 0, i == n - 1, sq_keys[i] + ["onesb"], PK(b))
            ACT(SLf(S_RT), ps[b][:, :], AF.Sqrt, PK(b) + ["epst"], K(S_RT, 2), scale=1.0 / nfeat, bias=epst[:, 0:1])
            RCP(out_rstd, SLf(S_RT), K(S_RT, 2), out_key)

        pstat = [None, None]

        def stat_begin():
            b = nb()
            reserved.add(b)
            pstat[par[0]] = b
            return b

        def stat_chunk(b, m, first, last):
            ACT(SL(S_SQ + m), xs(m), AF.Square, xk(m), K(S_SQ + m))
            return (b, m, first, last)

        def stat_mm(item):
            b, m, first, last = item
            MM(ps[b][:, :], onesb[:], SL(S_SQ + m), first, last, K(S_SQ + m) + ["onesb"], PK(b))

        early_rstd = [False, False]

        def rstd_from_pending(early=False):
            b = pstat[par[0]]
            rt, rs = (48, 46) if early else (S_RT, S_RSTD)
            ACT(SLf(rt), ps[b][:, :], AF.Sqrt, PK(b) + ["epst"], K(rt, 2), scale=1.0 / 1024.0, bias=epst[:, 0:1])
            RCP(SLf(rs), SLf(rt), K(rt, 2), K(rs, 2))
            reserved.discard(b)
            pstat[par[0]] = None
            early_rstd[par[0]] = early

        def norm_mod(cd, n):
            if early_rstd[par[0]]:
                rs = 46
                early_rstd[par[0]] = False
            else:
                rstd_from_pending()
                rs = S_RSTD
            for m in range(8):
                tk = S_TMP0 if m % 2 == 0 else S_TMP1
                TT("dve", SLf(tk), xs(m), SLf(rs), ALU.mult, xk(m) + K(rs, 2), K(tk, 2))
                ACT(hs(m), SLf(tk), AF.Identity, K(tk, 2) + ["dc"], K(S_H + m), scale=dcs(cd, n, 0, m), bias=dcs(cd, n, 1, m))

        def lin_multi(specs, KC, rhs_fn, rhs_keys):
            banks = [nb() for _ in specs]
            for kc in range(KC):
                for (wv_, wk, col0, ncol), b in zip(specs, banks):
                    MM(ps[b][0:ncol, :], wv_[:, kc, col0:col0 + ncol], rhs_fn(kc), kc == 0, kc == KC - 1,
                       wk + (rhs_keys(kc) if callable(rhs_keys) else rhs_keys), PK(b))
            return banks

        def ffn(cd, n, w1, w2, stats_after=True, mid_hook=None, skip_norm=False):
            if not skip_norm:
                norm_mod(cd, n)
            hk = lambda kc: K(S_H + kc)
            for blk in range(11):
                wt, wk = WLOAD(w1[blk])
                wv_ = wt[:].rearrange("p (k g c) -> p k (g c)", k=8, g=2)
                bks = lin_multi([(wv_, wk, gu * 256 + jj * 128, 128) for jj in range(2) for gu in range(2)], 8, hs, hk)
                for jj in range(2):
                    j = 2 * blk + jj
                    bg, bu = bks[jj * 2], bks[jj * 2 + 1]
                    sg = S_SG0 if j % 2 == 0 else S_SG1
                    ACT(SLf(sg), ps[bg][:, :], AF.Silu, PK(bg), K(sg, 2))
                    TT("dve", SL(S_A + j), SLf(sg), ps[bu][:, :], ALU.mult, K(sg, 2) + PK(bu), K(S_A + j))
            if stats_after:
                ACT(small[:, 30:31], epst[:, 0:1], AF.Sqrt, ["epst"], [("small", 30)])
            if mid_hook is not None:
                mid_hook()
            sb_ = stat_begin() if stats_after else None
            pend = None
            for m in range(8):
                wt, wk = WLOAD(w2[m], 2816)
                wv_ = wt[:, 0:2816].rearrange("p (j c) -> p j c", j=22)
                b = nb()
                for j in range(22):
                    MM(ps[b][:, :], wv_[:, j, :], SL(S_A + j), j == 0, j == 21, wk + K(S_A + j), PK(b))
                if pend is not None:
                    stat_mm(pend)
                STT(xs(m), ps[b][:, :], dcs(cd, n, 2, m), xs(m), ALU.mult, ALU.add, PK(b) + xk(m) + ["dc"], xk(m))
                if stats_after:
                    pend = stat_chunk(sb_, m, m == 0, m == 7)
            if pend is not None:
                stat_mm(pend)

        def lin_chunk(wv_, wk, col0, ncol, KC, rhs_fn, rhs_keys, bank=None):
            b = nb() if bank is None else bank
            for kc in range(KC):
                MM(ps[b][0:ncol, :], wv_[:, kc, col0:col0 + ncol], rhs_fn(kc), kc == 0, kc == KC - 1,
                   wk + (rhs_keys(kc) if callable(rhs_keys) else rhs_keys), PK(b))
            return b

        def qk_latent(cd, sample, koff=None, want_q=True, want_kv=True, cos_src=None, sin_src=None, g=None, mid_hook=None):
            hk = lambda kc: K(S_H + kc)
            specs = []
            if want_q:
                wt, wkq = WLOAD(D["wql"][:, :], 2048)
                wvq = wt[:, 0:2048].rearrange("p (k c) -> p k c", k=8)
                specs += [(wvq, wkq, 0, 128), (wvq, wkq, 128, 128)]
            if want_kv:
                wt, wkc = WLOAD(D["wck"][:, :], 2048)
                wvc = wt[:, 0:2048].rearrange("p (k c) -> p k c", k=8)
                specs += [(wvc, wkc, 0, 128), (wvc, wkc, 128, 128)]
                wt, wkr_ = WLOAD(D["wkr"][:, :], 1536)
                wvr = wt[:, 0:1536].rearrange("p (k c) -> p k c", k=8)
                if not want_q:
                    specs += [(wvr, wkr_, 0, 128), (wvr, wkr_, 64, 128)]
            banks = lin_multi(specs, 8, hs, hk)
            if want_q and want_kv:
                b1 = lin_chunk(wvr, wkr_, 0, 128, 8, hs, hk)
            hooked = [False]

            def run_hook():
                if mid_hook is not None and not hooked[0]:
                    hooked[0] = True
                    mid_hook()

            if want_q:
                for c in range(2):
                    b = banks.pop(0)
                    CP("act", SLf(S_QLR + 2 * c), ps[b][:, :], PK(b), K(S_QLR + 2 * c, 2))
                    ACT(SL(S_SQ + c), ps[b][:, :], AF.Square, PK(b), K(S_SQ + c))
                stats([SL(S_SQ + c) for c in range(2)], [K(S_SQ + c) for c in range(2)], 256.0, SLf(S_RSTD), K(S_RSTD, 2))
                for c in range(2):
                    STT(SL(S_QLN + c), SLf(S_QLR + 2 * c), gains2[:, c:c + 1], SLf(S_RSTD), ALU.mult, ALU.mult,
                        K(S_QLR + 2 * c, 2) + K(S_RSTD, 2) + ["gains2"], K(S_QLN + c))
            if want_kv:
                for c in range(2):
                    b = banks.pop(0)
                    CP("act", SLf(S_CKR + 2 * c), ps[b][:, :], PK(b), K(S_CKR + 2 * c, 2))
                    ACT(SL(S_SQ + 2 + c), ps[b][:, :], AF.Square, PK(b), K(S_SQ + 2 + c))
                if sample and not want_q:
                    b1 = banks.pop(0)
                    b2 = banks.pop(0)
                    DMA("sp", cost[:, :], cos_src, (), K(80, 2), "cost")
                    DMA("sp", sint[:, :], sin_src, (), K(82, 2), "sint")
                    TT("dve", SLf(S_TMP0, 0, 64), ps[b1][0:64, :], cost[:, :], ALU.mult, PK(b1) + K(80, 2), K(S_TMP0, 2))
                    TT("dve", SLf(S_SG0, 0, 64), ps[b2][0:64, :], sint[:, :], ALU.mult, PK(b2) + K(82, 2), K(S_SG0, 2))
                    TT("dve", krT[0:64, koff:koff + T], SLf(S_TMP0, 0, 64), SLf(S_SG0, 0, 64), ALU.add, K(S_TMP0, 2) + K(S_SG0, 2), [("krT", koff)])
                    run_hook()
                stats([SL(S_SQ + 2 + c) for c in range(2)], [K(S_SQ + 2 + c) for c in range(2)], 256.0, SLf(S_TMP1), K(S_TMP1, 2))
                for c in range(2):
                    if sample:
                        STT(ckvT[:, c * NK + koff:c * NK + koff + T], SLf(S_CKR + 2 * c), gains2[:, 2 + c:3 + c], SLf(S_TMP1), ALU.mult, ALU.mult,
                            K(S_CKR + 2 * c, 2) + K(S_TMP1, 2) + ["gains2"], [("ckvT", koff)])
                    else:
                        STT(SLf(S_CKR + 2 * c), SLf(S_CKR + 2 * c), gains2[:, 2 + c:3 + c], SLf(S_TMP1), ALU.mult, ALU.mult,
                            K(S_CKR + 2 * c, 2) + K(S_TMP1, 2) + ["gains2"], K(S_CKR + 2 * c, 2))
                        CP("act", SL(S_CKN + c), SLf(S_CKR + 2 * c), K(S_CKR + 2 * c, 2), K(S_CKN + c))
                if sample and not want_q:
                    return
                if sample:
                    b2 = banks.pop(0)
                    DMA("sp", cost[:, :], cos_src, (), K(80, 2), "cost")
                    DMA("sp", sint[:, :], sin_src, (), K(82, 2), "sint")
                    TT("dve", SLf(S_TMP0, 0, 64), ps[b1][0:64, :], cost[:, :], ALU.mult, PK(b1) + K(80, 2), K(S_TMP0, 2))
                    TT("dve", SLf(S_SG0, 0, 64), ps[b2][0:64, :], sint[:, :], ALU.mult, PK(b2) + K(82, 2), K(S_SG0, 2))
                    TT("dve", krT[0:64, koff:koff + T], SLf(S_TMP0, 0, 64), SLf(S_SG0, 0, 64), ALU.add, K(S_TMP0, 2) + K(S_SG0, 2), [("krT", koff)])
                else:
                    CP("act", SLf(S_KRF, 0, 64), ps[b1][0:64, :], PK(b1), K(S_KRF, 2))
                    CP("dve", krTp[0:64, :], SLf(S_KRF, 0, 64), K(S_KRF, 2), ["krTp"])
                    ost = arena[:, S_OST * 512:(S_OST + 4) * 512].bitcast(F32).rearrange("p (t f) -> p t f", t=4)
                    for t2 in range(2):
                        b = nb()
                        for ti in range(2):
                            tt = t2 * 2 + ti
                            for c in range(2):
                                TR(ps[b][:, (ti * 2 + c) * 128:(ti * 2 + c + 1) * 128], SLf(S_CKR + 2 * c)[:, tt * 128:(tt + 1) * 128], ident[:],
                                   K(S_CKR + 2 * c, 2) + ["ident"], PK(b))
                        CP("act", ost[:, t2 * 2:t2 * 2 + 2, :], ps[b][:, :].rearrange("p (t f) -> p t f", t=2), PK(b), K(S_OST, 4))
                    DMA("sp", D["nckv"][g * T:(g + 1) * T, :].rearrange("(t p) f -> p t f", p=128), ost, K(S_OST, 4), (), "ost")
                    ost2 = arena[:, S_OST2 * 512:(S_OST2 + 1) * 512].bitcast(F32).rearrange("p (t f) -> p t f", t=4)
                    b = nb()
                    for tt in range(4):
                        TR(ps[b][:, tt * 64:(tt + 1) * 64], SLf(S_KRF, 0, 64)[:, tt * 128:(tt + 1) * 128], ident[0:64, 0:64],
                           K(S_KRF, 2) + ["ident"], PK(b))
                    CP("dve", ost2, ps[b][:, 0:256].rearrange("p (t f) -> p t f", t=4), PK(b), K(S_OST2))
                    DMA("sp", D["nkr"][g * T:(g + 1) * T, :].rearrange("(t p) f -> p t f", p=128), ost2, K(S_OST2), (), "ost2")

        def q_proj(sample, cos_src=None, sin_src=None):
            wt, wk = WLOAD(D["wq"][:, :])
            wv_ = wt[:].rearrange("p (k c) -> p k c", k=2)
            qlk = K(S_QLN, 2)
            qln = lambda kc: SL(S_QLN + kc)
            MEMSET("pool", SL(S_QR, 8, 64, 128), 0.0, K(S_QR, 8) + ["qrz"])
            if sample:
                DMA("sp", cost[:, :], cos_src, (), K(80, 2), "cost")
                DMA("sp", sint[:, :], sin_src, (), K(82, 2), "sint")
            for h in range(8):
                b = lin_chunk(wv_, wk, h * 128, 128, 2, qln, qlk)
                CP("act" if h % 2 == 0 else "dve", SL(S_QN + h), ps[b][:, :], PK(b), K(S_QN + h))
            for h in range(8):
                b1 = lin_chunk(wv_, wk, 1024 + h * 64, 128, 2, qln, qlk)
                if sample:
                    b2 = lin_chunk(wv_, wk, 1536 + h * 64, 128 if h < 7 else 64, 2, qln, qlk)
                    TT("dve", SLf(S_TMP0, 0, 64), ps[b1][0:64, :], cost[:, :], ALU.mult, PK(b1) + K(80, 2), K(S_TMP0, 2))
                    TT("dve", SLf(S_SG0, 0, 64), ps[b2][0:64, :], sint[:, :], ALU.mult, PK(b2) + K(82, 2), K(S_SG0, 2))
                    TT("dve", SL(S_QR + h, 1, 0, 64), SLf(S_TMP0, 0, 64), SLf(S_SG0, 0, 64), ALU.add, K(S_TMP0, 2) + K(S_SG0, 2), K(S_QR + h))
                else:
                    CP("act", SL(S_QR + h, 1, 0, 64), ps[b1][0:64, :], PK(b1), K(S_QR + h))

        wkvv = wkv[:].rearrange("p (k c) -> p k c", k=2)

        NKV = NKT * 128

        def attn_prompt_pre():
            knT = kvb1[:, 0:8 * T].rearrange("p (h t) -> p h t", h=8)
            Vp = kvb1[:, 4096:4096 + 4 * 8 * 128].rearrange("p (t h c) -> p t h c", t=4, h=8)
            ckn = lambda kc: SL(S_CKN + kc)
            ckk = K(S_CKN, 2)
            for h in range(8):
                b = lin_chunk(wkvv, ["wkv"], h * 128, 128, 2, ckn, ckk)
                CP("act" if h % 2 == 0 else "dve", knT[:, h, :], ps[b][:, :], PK(b), [("knT", h)])
            for tt in range(4):
                for hh in range(2):
                    b = nb()
                    for kc in range(2):
                        MM(ps[b][:, :], SL(S_CKN + kc)[:, tt * 128:(tt + 1) * 128], wkvv[:, kc, 1024 + hh * 512:1024 + (hh + 1) * 512], kc == 0, kc == 1,
                           ckk + ["wkv"], PK(b))
                    CP("act" if hh == 0 else "dve", Vp[:, tt, hh * 4:(hh + 1) * 4, :], ps[b][:, :].rearrange("p (h c) -> p h c", h=4), PK(b), [("Vp", tt, hh)])

        def attn_prompt():
            knT = kvb1[:, 0:8 * T].rearrange("p (h t) -> p h t", h=8)
            Vp = kvb1[:, 4096:4096 + 4 * 8 * 128].rearrange("p (t h c) -> p t h c", t=4, h=8)
            pend = []

            def fin(item):
                it, sq, h, pt = item
                ptv = SL(pt).rearrange("p (k q) -> p k q", k=2)
                ob = nb()
                for kt in range(2):
                    MM(ps[ob][:, 0:256], Vp[:, 2 * sq + kt, h, :], ptv[:, kt, :], kt == 0, kt == 1, K(pt) + [("Vp", 2 * sq + kt, h // 4)], PK(ob))
                for kt in range(2):
                    MM(ps[ob][:, 256:512], onesb[:], ptv[:, kt, :], kt == 0, kt == 1, K(pt) + ["onesb"], PK(ob))
                tk = S_TMP0 if it % 2 == 0 else S_TMP1
                RCP(SLf(tk)[:, 0:256], ps[ob][:, 256:512], PK(ob), K(tk, 2))
                TT("dve", SL(S_OUTB + h)[:, sq * 256:(sq + 1) * 256], ps[ob][:, 0:256], SLf(tk)[:, 0:256], ALU.mult, PK(ob) + K(tk, 2), K(S_OUTB + h))

            it = 0
            for sq in range(2):
                for h in range(8):
                    sbk = nb()
                    sv = ps[sbk][:, :].rearrange("p (k q) -> p k q", k=2)
                    for kt in range(2):
                        tok0 = (2 * sq + kt) * 128
                        MM(sv[:, kt, :], knT[:, h, tok0:tok0 + 128], SL(S_QN + h)[:, sq * 256:(sq + 1) * 256], True, False,
                           [("knT", h)] + K(S_QN + h), PK(sbk))
                        MM(sv[:, kt, :], krTp[:, tok0:tok0 + 128], SL(S_QR + h)[:, sq * 256:(sq + 1) * 256], False, True,
                           ["krTp", "krTpz", "qrz"] + K(S_QR + h), PK(sbk))
                    pt = S_PT + (it % 4)
                    ACT(SL(pt), ps[sbk][:, :], AF.Exp, PK(sbk), K(pt), scale=SCALE)
                    pend.append((it, sq, h, pt))
                    if len(pend) > 2:
                        fin(pend.pop(0))
                    it += 1
            while pend:
                fin(pend.pop(0))

        KB = [kvb1[:, 0:NK], kK2[:, 0:NK]]

        NKB = (NK + 511) // 512

        def k_gen(h, par, only=None):
            KhT = KB[par]
            for kb in (range(NKB) if only is None else [only]):
                w = min(512, NK - kb * 512)
                b = nb()
                for kc in range(2):
                    MM(ps[b][:, 0:w], wkvv[:, kc, h * 128:(h + 1) * 128], ckvT[:, kc * NK + kb * 512:kc * NK + kb * 512 + w], kc == 0, kc == 1,
                       ["wkv", "ckvT_all"], PK(b))
                CP("act" if kb % 2 == 0 else "dve", KhT[:, kb * 512:kb * 512 + w], ps[b][:, 0:w], PK(b), [("kK", par)] + ([("kvb", 0)] if par == 0 else []))

        def v_gen(h):
            Vh = kvb1[:, NK:NK + NKV].rearrange("p (k c) -> p k c", c=128)
            for k0 in range(0, NKT, 4):
                n = min(4, NKT - k0)
                b = nb()
                for j in range(n):
                    kt = k0 + j
                    for kc in range(2):
                        MM(ps[b][:, j * 128:(j + 1) * 128], ckvT[:, kc * NK + kt * 128:kc * NK + (kt + 1) * 128], wkvv[:, kc, 1024 + h * 128:1024 + (h + 1) * 128],
                           kc == 0, kc == 1, ["wkv", "ckvT_all"], PK(b))
                CP("act", Vh[:, k0:k0 + n, :], ps[b][:, 0:n * 128].rearrange("p (k c) -> p k c", c=128), PK(b), [("kV",), ("kvb", 0)])

        def attn_pre():
            k_gen(0, 0)
            v_gen(0)

        def attn_sample():
            for b_ in (4, 5, 6, 7):
                reserved.add(b_)
            it = 0
            PT_RING = [52, 53, 54, 55, 56, 57, 60, 61, 62, 63]
            accD, accP = S_SG0, S_SG1
            for h in range(8):
                hp = h % 2
                KhT = KB[hp]
                Vh = kvb1[:, NK:NK + NKV].rearrange("p (k c) -> p k c", c=128)
                ot, dn = 4 + hp, 6 + hp
                pend = []
                nD = [0]
                nP = [0]
                kstep = max(1, NKT // NKB)
                knext = [0]

                def fin(item):
                    kt, pt = item
                    MM(ps[ot][:, :], Vh[:, kt, :], SL(pt), kt == 0, kt == NKT - 1, K(pt) + [("kV",)], PK(ot))
                    if kt % 2 == 1:
                        if nP[0] == 0:
                            CP("pool", SLf(accP), SL(pt), K(pt), K(accP, 2))
                        else:
                            TT("pool", SLf(accP), SL(pt), SLf(accP), ALU.add, K(pt) + K(accP, 2), K(accP, 2))
                        nP[0] += 1
                    else:
                        if nD[0] == 0:
                            CP("dve", SLf(accD), SL(pt), K(pt), K(accD, 2))
                        else:
                            TT("dve", SLf(accD), SL(pt), SLf(accD), ALU.add, K(pt) + K(accD, 2), K(accD, 2))
                        nD[0] += 1

                for kt in range(NKT):
                    sbk = nb()
                    MM(ps[sbk][:, :], KhT[:, kt * 128:(kt + 1) * 128], SL(S_QN + h), True, False, [("kK", hp)] + K(S_QN + h), PK(sbk))
                    MM(ps[sbk][:, :], krT[:, kt * 128:(kt + 1) * 128], SL(S_QR + h), False, True,
                       ["krT_all", "krTz", "qrz"] + K(S_QR + h), PK(sbk))
                    pt = PT_RING[it % len(PT_RING)]
                    it += 1
                    ACT(SL(pt), ps[sbk][:, :], AF.Exp, PK(sbk), K(pt), scale=SCALE)
                    pend.append((kt, pt))
                    if len(pend) > 2:
                        fin(pend.pop(0))
                    if h < 7 and kt % kstep == 1 and knext[0] < NKB:
                        k_gen(h + 1, 1 - hp, only=knext[0])
                        knext[0] += 1
                while pend:
                    fin(pend.pop(0))
                while h < 7 and knext[0] < NKB:
                    k_gen(h + 1, 1 - hp, only=knext[0])
                    knext[0] += 1
                tk = S_TMP0 if hp == 0 else S_TMP1
                if nP[0] > 0:
                    TT("dve", SLf(accD), SLf(accD), SLf(accP), ALU.add, K(accD, 2) + K(accP, 2), K(accD, 2))
                hi, lo = S_PT + 6, S_PT + 7
                CP("dve", SL(hi), SLf(accD), K(accD, 2), K(hi))
                TT("dve", SL(lo), SLf(accD), SL(hi), ALU.subtract, K(accD, 2) + K(hi), K(lo))
                if h < 7:
                    v_gen(h + 1)
                MM(ps[dn][:, :], onesb[:], SL(hi), True, False, K(hi) + ["onesb"], PK(dn))
                MM(ps[dn][:, :], onesb[:], SL(lo), False, True, K(lo) + ["onesb"], PK(dn))
                RCP(SLf(tk), ps[dn][:, :], PK(dn), K(tk, 2))
                TT("dve", SL(S_OUTB + h), ps[ot][:, :], SLf(tk), ALU.mult, PK(ot) + K(tk, 2), K(S_OUTB + h))
            for b_ in (4, 5, 6, 7):
                reserved.discard(b_)

        def gmlp_prefetch():
            return [WLOAD(D["wu"][i]) for i in range(2)], [WLOAD(D["wv"][i]) for i in range(2)]

        def gmlp_merge_out(cd, pre=None):
            hk = K(S_H, 8)
            wts = pre[0] if pre is not None else [WLOAD(D["wu"][i]) for i in range(2)]
            for g in range(8):
                wt, wk = wts[g // 4]
                wv_ = wt[:].rearrange("p (k c) -> p k c", k=8)
                b = lin_chunk(wv_, wk, (g % 4) * 128, 128, 8, hs, hk)
                CP("act", SL(S_U + g), ps[b][:, :], PK(b), K(S_U + g))
            wts = pre[1] if pre is not None else [WLOAD(D["wv"][i]) for i in range(2)]
            for tt in range(4):
                bb = []
                for hh in range(2):
                    wt, wk = wts[hh]
                    wv_ = wt[:].rearrange("p (k c) -> p k c", k=8)
                    b = nb()
                    for kc in range(8):
                        MM(ps[b][:, :], hs(kc)[:, tt * 128:(tt + 1) * 128], wv_[:, kc, :], kc == 0, kc == 7, wk + hk, PK(b))
                    ACT(SL(S_SQ + hh), ps[b][:, :], AF.Square, PK(b), K(S_SQ + hh) + [("small", 2 + hh)], accum=small[:, 8 + hh:9 + hh])
                    bb.append(b)
                TT("dve", small[:, 10:11], small[:, 8:9], small[:, 9:10], ALU.add, [("small", 2), ("small", 3)], [("small", 4)])
                ACT(small[:, 11:12], small[:, 10:11], AF.Sqrt, [("small", 4), "epst"], [("small", 5)], scale=1.0 / 1024, bias=epst[:, 0:1])
                RCP(small[:, 12:13], small[:, 11:12], [("small", 5)], [("small", 6)])
                for hh in range(2):
                    STT(SL(S_VV + tt * 2 + hh), ps[bb[hh]][:, :], small[:, 12:13], gvbc[:, hh * 512:(hh + 1) * 512], ALU.mult, ALU.mult,
                        PK(bb[hh]) + [("small", 6), "gvbc"], K(S_VV + tt * 2 + hh))
            wsv = wsT[:].rearrange("p (g q) -> p g q", g=8)
            for g in range(8):
                b = nb()
                for tt in range(4):
                    vsl = S_VV + tt * 2 + g // 4
                    MM(ps[b][:, tt * 128:(tt + 1) * 128], SL(vsl)[:, (g % 4) * 128:(g % 4 + 1) * 128], wsv[:, g, :], True, True,
                       K(vsl) + ["wsT"], PK(b))
                tk = S_TMP0 if g % 2 == 0 else S_TMP1
                TT("dve", SLf(tk).rearrange("p (t q) -> p t q", t=4), ps[b][:, :].rearrange("p (t q) -> p t q", t=4),
                   bsbc[:, g * 128:(g + 1) * 128].unsqueeze(1).to_broadcast([128, 4, 128]), ALU.add, PK(b) + ["bsbc"], K(tk, 2))
                TT("dve", SL(S_U + g), SLf(tk), SL(S_U + g), ALU.mult, K(tk, 2) + K(S_U + g), K(S_U + g))
            for bi, br in enumerate((1, 0)):
                wsrc = D["wa"] if br == 0 else D["wb"]
                insl = S_U if br == 0 else S_OUTB
                for half in range(2):
                    wtp, wkp = WLOAD(wsrc[half])
                    wtg, wkg = WLOAD(D["wg"][br * 2 + half])
                    wvp = wtp[:].rearrange("p (k c) -> p k c", k=8)
                    wvg = wtg[:].rearrange("p (k c) -> p k c", k=8)
                    for mm_ in range(4):
                        m = half * 4 + mm_
                        bt = lin_chunk(wvp, wkp, mm_ * 128, 128, 8, lambda kc: SL(insl + kc), K(insl, 8))
                        bl = lin_chunk(wvg, wkg, mm_ * 128, 128, 8, hs, hk)
                        sg = S_SG0 if m % 2 == 0 else S_SG1
                        ACT(SLf(sg), ps[bl][:, :], AF.Sigmoid, PK(bl), K(sg, 2))
                        if bi == 0:
                            TT("dve", SL(S_MRG + m), SLf(sg), ps[bt][:, :], ALU.mult, K(sg, 2) + PK(bt), K(S_MRG + m))
                        else:
                            tk = S_TMP0 if m % 2 == 0 else S_TMP1
                            TT("dve", SLf(tk), SLf(sg), ps[bt][:, :], ALU.mult, K(sg, 2) + PK(bt), K(tk, 2))
                            TT("dve", SL(S_MRG + m), SLf(tk), SL(S_MRG + m), ALU.add, K(tk, 2) + K(S_MRG + m), K(S_MRG + m))
            ACT(small[:, 30:31], epst[:, 0:1], AF.Sqrt, ["epst"], [("small", 30)])
            sb_ = stat_begin()
            pend = None
            for half in range(2):
                wt, wk = WLOAD(D["wo"][half])
                wv_ = wt[:].rearrange("p (k c) -> p k c", k=8)
                for mm_ in range(4):
                    m = half * 4 + mm_
                    b = lin_chunk(wv_, wk, mm_ * 128, 128, 8, lambda kc: SL(S_MRG + kc), K(S_MRG, 8))
                    if pend is not None:
                        stat_mm(pend)
                    STT(xs(m), ps[b][:, :], dcs(cd, 1, 2, m), xs(m), ALU.mult, ALU.add, PK(b) + xk(m) + ["dc"], xk(m))
                    pend = stat_chunk(sb_, m, m == 0, m == 7)
            stat_mm(pend)

        def load_gf():
            DMA("sp", arena[:, 48 * 512:52 * 512].bitcast(F32), D["gfbc"][:, :], (), K(48, 4), "gfbc")

        def final_store(dst_rows):
            gfv = arena[:, 48 * 512:52 * 512].bitcast(F32)
            for tt in range(4):
                pc = 16 + 5 * (tt % 2)
                so = S_XIN + 4 * (tt % 2)
                stg = arena[:, so * 512:(so + 4) * 512].bitcast(F32)
                bb = []
                for half in range(2):
                    b = nb()
                    for k in range(4):
                        m = half * 4 + k
                        TR(ps[b][:, k * 128:(k + 1) * 128], xs(m)[:, tt * 128:(tt + 1) * 128], ident[:], xk(m) + ["ident"], PK(b))
                    ACT(SL(S_SQ + 4 * (tt % 2) + half), ps[b][:, :], AF.Square, PK(b), K(S_SQ + 4 * (tt % 2) + half) + [("small", pc + half)],
                        accum=small[:, pc + half:pc + half + 1])
                    bb.append(b)
                TT("dve", small[:, pc + 2:pc + 3], small[:, pc:pc + 1], small[:, pc + 1:pc + 2], ALU.add, [("small", pc), ("small", pc + 1)], [("small", pc + 2)])
                ACT(small[:, pc + 3:pc + 4], small[:, pc + 2:pc + 3], AF.Sqrt, [("small", pc + 2), "epst"], [("small", pc + 3)], scale=1.0 / 1024, bias=epst[:, 0:1])
                RCP(small[:, pc + 4:pc + 5], small[:, pc + 3:pc + 4], [("small", pc + 3)], [("small", pc + 4)])
                for half in range(2):
                    STT(stg[:, half * 512:(half + 1) * 512], ps[bb[half]][:, :], small[:, pc + 4:pc + 5], gfv[:, half * 512:(half + 1) * 512], ALU.mult, ALU.mult,
                        PK(bb[half]) + [("small", pc + 4)] + K(48, 4), K(so + 2 * half, 2))
                DMA("sp", dst_rows[tt * 128:(tt + 1) * 128, :], stg, K(so, 4), (), ("xin", tt % 2))

        def reload_x(i):
            DMA("sp", xall(), xs1[i], [("xs1", i)], xallk(), ("xs1", i))
            sb_ = stat_begin()
            pend = None
            for m in range(8):
                it_ = stat_chunk(sb_, m, m == 0, m == 7)
                if pend is not None:
                    stat_mm(pend)
                pend = it_
            stat_mm(pend)

        groups = []
        for g in range(NPH):
            groups.append(dict(kind="P", idx=g, cd=0, n0=0, load=(lambda g=g: load_x(D["xp"][g * T:(g + 1) * T, :]))))
        for i in range(NOH + NOX):
            own = i < NOH
            ii = i if own else i - NOH
            src = D["xso"] if own else D["xsx"]
            groups.append(dict(kind="A", idx=i, cd=1, n0=0, load=(lambda src=src, ii=ii: load_x(src[ii * T:(ii + 1) * T, :]))))
        for i in range(NOH):
            groups.append(dict(kind="B", idx=i, cd=1, n0=1, load=(lambda i=i: reload_x(i))))

        USE_EARLY = False

        def prefetch(gi, with_norm):
            if gi >= len(groups):
                return
            grp = groups[gi]
            if grp["kind"] == "B" and groups[gi - 1]["kind"] == "A":
                return
            save = par[0]
            par[0] = gi % 2
            if not grp.get("loaded"):
                grp["load"]()
                grp["loaded"] = True
                if not with_norm and USE_EARLY:
                    rstd_from_pending(early=True)
            if with_norm and not grp.get("normed"):
                norm_mod(grp["cd"], grp["n0"])
                grp["normed"] = True
            par[0] = save

        did_ctx = False
        did_barrier = False
        for gi, grp in enumerate(groups):
            par[0] = gi % 2
            kind, cd = grp["kind"], grp["cd"]
            if kind == "A" and not did_ctx:
                did_ctx = True
                pk = [("knT", h) for h in range(8)] + [("Vp", tt, hh) for tt in range(4) for hh in range(2)]
                MEMSET("dve", small[:, 13:14], 0.0, pk + [("kvb", 0)])
                for kt in range(PAST // 128):
                    so = S_XIN + 4 * (kt % 2)
                    stg = arena[:, so * 512:(so + 4) * 512].bitcast(F32)
                    DMA("sp", stg[:, 0:256], D["cck"][kt * 128:(kt + 1) * 128, :], (), K(so, 2), ("xin", kt % 2))
                    DMA("sp", stg[:, 512:576], D["ckr"][kt * 128:(kt + 1) * 128, :], (), K(so + 2, 1), ("xinb", kt % 2))
                    b = nb()
                    for c in range(2):
                        TR(ps[b][:, c * 128:(c + 1) * 128], stg[:, c * 128:(c + 1) * 128], ident[:], K(so, 2) + ["ident"], PK(b))
                    TR(ps[b][0:64, 256:384], stg[:, 512:576], ident[:], K(so + 2, 1) + ["ident"], PK(b))
                    for c in range(2):
                        CP("act", ckvT[:, c * NK + kt * 128:c * NK + (kt + 1) * 128], ps[b][:, c * 128:(c + 1) * 128], PK(b), [("ckvT", -1 - kt)])
                    CP("act", krT[0:64, kt * 128:(kt + 1) * 128], ps[b][0:64, 256:384], PK(b), [("krT", -1 - kt)])
            if kind == "B" and not did_barrier:
                did_barrier = True
                allk = [("ckvT", PAST + i * T) for i in range(NOH + NOX)] + [("ckvT", -1 - kt) for kt in range(PAST // 128)]
                allr = [("krT", PAST + i * T) for i in range(NOH + NOX)] + [("krT", -1 - kt) for kt in range(PAST // 128)]
                P.add("dve", lambda e: e.memset(small[:, 15:16], 0.0), allk, ["ckvT_all"])
                P.add("dve", lambda e: e.memset(small[:, 14:15], 0.0), allr, ["krT_all"])
            if not grp.get("loaded"):
                grp["load"]()
            normed = grp.get("normed", False)
            if kind == "P":
                g = grp["idx"]
                ffn(0, 0, D["f1w1"], D["f1w2"], skip_norm=normed)
                norm_mod(0, 1)
                qk_latent(0, False, g=g)
                attn_prompt_pre()
                q_proj(False)
                attn_prompt()
                gmlp_merge_out(0)
                load_gf()
                ffn(0, 2, D["f2w1"], D["f2w2"], stats_after=False, mid_hook=(lambda gi=gi: prefetch(gi + 1, True)))
                final_store(D["yp"][g * T:(g + 1) * T, :])
            elif kind == "A":
                i = grp["idx"]
                own = i < NOH
                ii = i if own else i - NOH
                ffn(1, 0, D["f1w1"], D["f1w2"], skip_norm=normed, mid_hook=(lambda gi=gi: prefetch(gi + 1, False)))
                if own:
                    DMA("sp", xs1[ii], xall(), xallk(), [("xs1", ii)], ("xs1", ii))
                norm_mod(1, 1)
                koff = PAST + i * T
                cs = (D["cos_o"] if own else D["cos_x"])[:, ii * T:(ii + 1) * T]
                sn = (D["sin_o"] if own else D["sin_x"])[:, ii * T:(ii + 1) * T]
                qk_latent(1, True, koff=koff, want_q=False, cos_src=cs, sin_src=sn, mid_hook=(lambda gi=gi: prefetch(gi + 1, True)))
            else:
                i = grp["idx"]
                if not normed:
                    norm_mod(1, 1)
                qk_latent(1, True, want_q=True, want_kv=False)
                attn_pre()
                q_proj(True, D["cos_o"][:, i * T:(i + 1) * T], D["sin_o"][:, i * T:(i + 1) * T])
                pre = gmlp_prefetch()
                attn_sample()
                gmlp_merge_out(1, pre)
                load_gf()
                ffn(1, 2, D["f2w1"], D["f2w2"], stats_after=False, mid_hook=(lambda gi=gi: prefetch(gi + 1, True)))
                final_store(D["ys"][i * T:(i + 1) * T, :])
        P.emit(st)
    return nc


def _kmaj(W):
    Kd, N = W.shape
    return np.ascontiguousarray(W.reshape(Kd // 128, 128, N).transpose(1, 0, 2)).reshape(128, (Kd // 128) * N)


def _blocks(W, bc):
    Kd, N = W.shape
    KC = Kd // 128
    return np.ascontiguousarray(W.reshape(KC, 128, N // bc, bc).transpose(2, 1, 0, 3)).reshape(N // bc, 128, KC * bc)


def _pp(v):
    return np.ascontiguousarray(v.reshape(-1, 128).T)


_SWAP = np.concatenate([np.arange(16, 32), np.arange(0, 16), np.arange(48, 64), np.arange(32, 48)])


def prep_shared(inp):
    f = lambda a: np.asarray(a, dtype=np.float32)
    sh = {}
    sh["modw"] = _blocks(f(inp["mod_w"])[0], 512)
    sh["modb"] = _pp(f(inp["mod_b"])[0])
    sh["gains"] = np.concatenate([_pp(f(inp["norm_ffn1"])[0]), _pp(f(inp["norm_mix"])[0]), _pp(f(inp["norm_ffn2"])[0]), _pp(f(inp["norm_final"]))], axis=1)
    sh["gains2"] = np.concatenate([_pp(f(inp["q_norm"])[0]), _pp(f(inp["kv_norm"])[0])], axis=1)
    sh["gvbc"] = np.ascontiguousarray(np.broadcast_to(f(inp["gmlp_v_norm"])[0][None, :], (128, 1024)))
    sh["gfbc"] = np.ascontiguousarray(np.broadcast_to(f(inp["norm_final"])[None, :], (128, 1024)))
    bs = f(inp["gmlp_b_s"])[0]
    sh["bsbc"] = np.ascontiguousarray(np.broadcast_to(bs.T.reshape(1, 1024), (128, 1024)))
    ws = f(inp["gmlp_w_s"])[0]
    sh["wsT"] = np.ascontiguousarray(ws.transpose(2, 0, 1)).reshape(128, 1024)
    sh["ident"] = np.eye(128, dtype=np.float32)
    for nm, key_in, key_out in (("f1", "ffn1_w_in", "ffn1_w_out"), ("f2", "ffn2_w_in", "ffn2_w_out")):
        W1 = f(inp[key_in])[0]
        sh[nm + "w1"] = np.ascontiguousarray(W1.reshape(8, 128, 2, 11, 256).transpose(3, 1, 0, 2, 4)).reshape(11, 128, 4096)
        W2 = f(inp[key_out])[0]
        sh[nm + "w2"] = np.ascontiguousarray(W2.reshape(22, 128, 8, 128).transpose(2, 1, 0, 3)).reshape(8, 128, 2816)
    Wi = f(inp["w_in"])[0]
    sh["wu"] = _blocks(Wi[:, 0:1024], 512)
    sh["wv"] = _blocks(Wi[:, 1024:2048], 512)
    sh["wql"] = _kmaj(Wi[:, 2048:2304])
    sh["wck"] = _kmaj(Wi[:, 2304:2560])
    kr = Wi[:, 2560:2624]
    sh["wkr"] = _kmaj(np.concatenate([kr, kr[:, _SWAP], kr], axis=1))
    sh["wg"] = _blocks(Wi[:, 2624:4672], 512)
    Wq = f(inp["w_q_up"])[0].reshape(256, 8, 192)
    qn = Wq[:, :, 0:128].reshape(256, 1024)
    qr = Wq[:, :, 128:192]
    sh["wq"] = _kmaj(np.concatenate([qn, qr.reshape(256, 512), qr[:, :, _SWAP].reshape(256, 512)], axis=1))
    Wkv = f(inp["w_kv_up"])[0].reshape(256, 8, 256)
    sh["wkv"] = _kmaj(np.concatenate([Wkv[:, :, 0:128].reshape(256, 1024), Wkv[:, :, 128:256].reshape(256, 1024)], axis=1))
    sh["wa"] = _blocks(f(inp["w_a_proj"])[0], 512)
    sh["wb"] = _blocks(f(inp["w_b_proj"])[0], 512)
    sh["wo"] = _blocks(f(inp["w_o"])[0], 512)
    return sh


def rope_tables(pos):
    pos = np.asarray(pos)
    r = (pos // 64).astype(np.float32)
    c = (pos % 64).astype(np.float32)
    inv = (1.0 / (np.float32(10000.0) ** (np.arange(0, 32, 2, dtype=np.float32) / np.float32(32)))).astype(np.float32)
    ang_r = r[None, :] * inv[:, None]
    ang_c = c[None, :] * inv[:, None]
    cos = np.concatenate([np.cos(ang_r), np.cos(ang_r), np.cos(ang_c), np.cos(ang_c)], axis=0)
    sin = np.concatenate([-np.sin(ang_r), np.sin(ang_r), -np.sin(ang_c), np.sin(ang_c)], axis=0)
    return np.ascontiguousarray(cos.astype(np.float32)), np.ascontiguousarray(sin.astype(np.float32))


def run_cfg(inp, n_cores, NPH, L):
    f = lambda a: np.asarray(a, dtype=np.float32)
    halfL = L // 2
    NOH = halfL // T
    NOX = NOH
    sh = prep_shared(inp)
    xp = f(inp["x_prompt"]).reshape(-1, 1024)
    xsm = f(inp["x_sample"])
    cc = f(inp["c"])
    cctx = f(inp["c_ctx"])
    cck = f(inp["cache_ckv"])
    ckr = f(inp["cache_krope"])
    in_maps = []
    for core in range(n_cores):
        s, hf = core // 2, core % 2
        m = dict(sh)
        m["xp"] = np.ascontiguousarray(xp[core * NPH * T:(core + 1) * NPH * T])
        own = np.arange(hf * halfL, (hf + 1) * halfL)
        oth = np.arange((1 - hf) * halfL, (2 - hf) * halfL)
        m["xso"] = np.ascontiguousarray(xsm[s, own])
        m["xsx"] = np.ascontiguousarray(xsm[s, oth])
        m["cv"] = np.ascontiguousarray(np.stack([_pp(cctx), _pp(cc[s])], axis=2)).reshape(128, 16)
        m["cck"] = np.ascontiguousarray(cck[s, 0])
        m["ckr"] = np.ascontiguousarray(ckr[s, 0])
        m["cos_o"], m["sin_o"] = rope_tables(own)
        m["cos_x"], m["sin_x"] = rope_tables(oth)
        in_maps.append(m)
    nc = build_program(NPH, NOH, NOX)
    res = run_bass_kernel_spmd(nc, in_maps, core_ids=list(range(n_cores)))
    r = res.results
    yp = np.concatenate([r[c]["yp"] for c in range(n_cores)], axis=0)
    nckv = np.concatenate([r[c]["nckv"] for c in range(n_cores)], axis=0)
    nkr = np.concatenate([r[c]["nkr"] for c in range(n_cores)], axis=0)
    ys = np.stack([np.concatenate([r[2 * s]["ys"], r[2 * s + 1]["ys"]], axis=0) for s in range(n_cores // 2)], axis=0)
    return yp, ys, nckv, nkr


def kernel(**inputs):
    B, S = inputs["x_prompt"].shape[0], inputs["x_prompt"].shape[1]
    yp, ys, nckv, nkr = run_cfg(inputs, 8, (B * S) // (8 * T), inputs["x_sample"].shape[1])
    return (yp.reshape(B, S, 1024).astype(np.float32), ys.astype(np.float32),
            nckv.reshape(B, 1, S, 256).astype(np.float32), nkr.reshape(B, 1, S, 64).astype(np.float32))
```
